# Optimizing a Trainium2 kernel written in Bass

```python
import jax, jax.numpy as jnp
from jax import lax
import numpy as np


D_MODEL = 1024
BATCH = 2
SEQ = 8192
DEPTH = 2

GRID_W = 64
CTX_LEN = 256
N_MOD = 9
NA_HEADS = 8
HEAD_DIM = 64
NA_WIDTH = NA_HEADS * HEAD_DIM
NA_KH = 8
NA_KW = 16
POOL_GROUPS = 4
POOL_CH = 128
POOL_WIDTH = POOL_GROUPS * POOL_CH
POOL_WINDOWS = (2, 4, 8, 16)
MIX_WIDTH = NA_WIDTH + POOL_WIDTH
IN_WIDTH = 3 * NA_WIDTH + POOL_WIDTH
D_FF = 2816
ROPE_THETA = 10000.0
ROPE_PAIRS = HEAD_DIM // 4
RMS_EPS = 1e-6
NEG_INF = -1e30

kernel_name = 'hybrid_na_pool_macaron_dit_block'


def rms_norm(x, g):
    xf = x.astype(jnp.float32)
    y = xf * lax.rsqrt(jnp.mean(xf * xf, axis=-1, keepdims=True) + RMS_EPS)
    return (y * g.astype(jnp.float32)).astype(x.dtype)


def modulate(h, shift, scale):
    return h * (1.0 + scale) + shift


def mod_vectors(cvec, w_mod, b_mod):
    m = jax.nn.silu(cvec) @ w_mod + b_mod
    return jnp.split(m, N_MOD, axis=-1)


def swiglu(h, w_gate_up, w_down):
    gate, up = jnp.split(h @ w_gate_up, 2, axis=-1)
    return (jax.nn.silu(gate) * up) @ w_down


def sandwich_ffn(x, w_gate_up, w_down, g_pre, g_post, shift, scale, gate):
    h = modulate(rms_norm(x, g_pre), shift, scale)
    return x + 0.5 * gate * rms_norm(swiglu(h, w_gate_up, w_down), g_post)


def axial_rope(x, rows):
    seq = rows * GRID_W
    t = jnp.arange(seq)
    inv = ROPE_THETA ** (-jnp.arange(ROPE_PAIRS, dtype=jnp.float32) / ROPE_PAIRS)

    def rot(xa, pos):
        ang = pos.astype(jnp.float32)[:, None] * inv
        cos = jnp.cos(ang)[:, None, :]
        sin = jnp.sin(ang)[:, None, :]
        x1, x2 = jnp.split(xa, 2, axis=-1)
        return jnp.concatenate([x1 * cos - x2 * sin, x2 * cos + x1 * sin], axis=-1)

    xr, xc = jnp.split(x.astype(jnp.float32), 2, axis=-1)
    return jnp.concatenate([rot(xr, t // GRID_W), rot(xc, t % GRID_W)], axis=-1).astype(x.dtype)


def neighbourhood_attention(q, k, v, kc, vc, rpb, rows):
    b, s, h, dh = q.shape
    kh = min(NA_KH, rows)
    scale = dh ** -0.5
    q = q.reshape(b, rows, GRID_W, h, dh)
    k = k.reshape(b, rows, GRID_W, h, dh)
    v = v.reshape(b, rows, GRID_W, h, dh)
    r = np.arange(rows)
    row_start = np.clip(r - kh // 2, 0, rows - kh)
    row_idx = row_start[:, None] + np.arange(kh)[None, :]
    k_blk = k[:, row_idx]
    v_blk = v[:, row_idx]
    j = np.arange(GRID_W)
    col_start = np.clip(j - NA_KW // 2, 0, GRID_W - NA_KW)
    col_valid = (j[None, :] >= col_start[:, None]) & (j[None, :] < col_start[:, None] + NA_KW)
    dr = row_idx - r[:, None]
    dc = np.clip(j[None, :] - j[:, None] + NA_KW - 1, 0, 2 * NA_KW - 2)
    bias = rpb[:, dr + NA_KH - 1]
    bias = bias[..., dc]
    bias = jnp.transpose(bias, (0, 1, 3, 2, 4)).astype(jnp.float32)
    s_loc = jnp.einsum('brqhd,brkchd->bhrqkc', q, k_blk, preferred_element_type=jnp.float32) * scale
    s_loc = jnp.where(col_valid[:, None, :], s_loc + bias, NEG_INF)
    s_ctx = jnp.einsum('brqhd,bkhd->bhrqk', q, kc, preferred_element_type=jnp.float32) * scale
    n_loc = kh * GRID_W
    scores = jnp.concatenate([s_loc.reshape(b, h, rows, GRID_W, n_loc), s_ctx], axis=-1)
    p = jax.nn.softmax(scores, axis=-1).astype(v.dtype)
    p_loc = p[..., :n_loc].reshape(b, h, rows, GRID_W, kh, GRID_W)
    p_ctx = p[..., n_loc:]
    o = jnp.einsum('bhrqkc,brkchd->brqhd', p_loc, v_blk) + jnp.einsum('bhrqk,bkhd->brqhd', p_ctx, vc)
    return o.reshape(b, s, h * dh)


def context_attention(q, k, v):
    b, l, h, dh = q.shape
    sc = jnp.einsum('bqhd,bkhd->bhqk', q, k, preferred_element_type=jnp.float32) * dh ** -0.5
    p = jax.nn.softmax(sc, axis=-1).astype(v.dtype)
    return jnp.einsum('bhqk,bkhd->bqhd', p, v).reshape(b, l, h * dh)


def pool_mix(u, w_pool, pool_scale):
    length = u.shape[-2]
    ug = u.reshape(u.shape[:-1] + (POOL_GROUPS, POOL_CH))
    uf = ug.astype(jnp.float32)
    cs = jnp.cumsum(uf, axis=-3)
    pad = [(0, 0)] * cs.ndim
    pad[-3] = (1, 0)
    cs = jnp.pad(cs, pad)
    t = np.arange(length)
    outs = []
    for g, w in enumerate(POOL_WINDOWS):
        lo = np.clip(t - w // 2, 0, length)
        hi = np.clip(t - w // 2 + w, 0, length)
        cnt = (hi - lo).astype(np.float32)[:, None]
        csg = cs[..., g, :]
        outs.append((jnp.take(csg, hi, axis=-2) - jnp.take(csg, lo, axis=-2)) / cnt)
    pooled = jnp.stack(outs, axis=-2)
    d = (pooled - uf).astype(u.dtype)
    y = jnp.einsum('...gc,gcd->...gd', d, w_pool) * pool_scale.reshape(POOL_GROUPS, POOL_CH)
    return y.reshape(u.shape)


def token_mix(hx, hc, w_in, w_out, rpb, w_pool, pool_scale, with_ctx_out):
    b, s, _ = hx.shape
    rows = s // GRID_W
    l = hc.shape[1]
    qx, kx, vx, ux = jnp.split(hx @ w_in, [NA_WIDTH, 2 * NA_WIDTH, 3 * NA_WIDTH], axis=-1)
    if with_ctx_out:
        qc, kc, vc, uc = jnp.split(hc @ w_in, [NA_WIDTH, 2 * NA_WIDTH, 3 * NA_WIDTH], axis=-1)
    else:
        kc, vc = jnp.split(hc @ w_in[:, NA_WIDTH:3 * NA_WIDTH], 2, axis=-1)
    heads = lambda t, n: t.reshape(b, n, NA_HEADS, HEAD_DIM)
    qx = axial_rope(heads(qx, s), rows)
    kx = axial_rope(heads(kx, s), rows)
    kc_h, vc_h = heads(kc, l), heads(vc, l)
    na_x = neighbourhood_attention(qx, kx, heads(vx, s), kc_h, vc_h, rpb, rows)
    pool_x = pool_mix(ux.reshape(b, rows, GRID_W, POOL_WIDTH), w_pool, pool_scale).reshape(b, s, POOL_WIDTH)
    out_x = jnp.concatenate([na_x, pool_x], axis=-1) @ w_out
    if with_ctx_out:
        na_c = context_attention(heads(qc, l), kc_h, vc_h)
        pool_c = pool_mix(uc, w_pool, pool_scale)
        out_c = jnp.concatenate([na_c, pool_c], axis=-1) @ w_out
        return out_x, out_c
    return out_x, None


def setup_inputs(seed: int = 0) -> dict:
    key = jax.random.key(seed)
    ks = jax.random.split(key, 14)
    nrm = jax.random.normal
    f32 = jnp.float32
    return {
        'x': nrm(ks[0], (BATCH, SEQ, D_MODEL), f32),
        'c': nrm(ks[1], (BATCH, D_MODEL), f32),
        'ctx': nrm(ks[2], (BATCH, CTX_LEN, D_MODEL), f32),
        'c_ctx': nrm(ks[3], (D_MODEL,), f32),
        'w_mod': nrm(ks[4], (DEPTH, D_MODEL, N_MOD * D_MODEL), f32) * (0.5 * D_MODEL ** -0.5),
        'b_mod': nrm(ks[5], (DEPTH, N_MOD * D_MODEL), f32) * 0.01,
        'norm_g': 1.0 + 0.05 * nrm(ks[6], (DEPTH, 6, D_MODEL), f32),
        'w_ffn_gate_up': nrm(ks[7], (DEPTH, 2, D_MODEL, 2 * D_FF), f32) * D_MODEL ** -0.5,
        'w_ffn_down': nrm(ks[8], (DEPTH, 2, D_FF, D_MODEL), f32) * D_FF ** -0.5,
        'w_in': nrm(ks[9], (DEPTH, D_MODEL, IN_WIDTH), f32) * D_MODEL ** -0.5,
        'w_out': nrm(ks[10], (DEPTH, MIX_WIDTH, D_MODEL), f32) * MIX_WIDTH ** -0.5,
        'na_rpb': nrm(ks[11], (DEPTH, NA_HEADS, 2 * NA_KH - 1, 2 * NA_KW - 1), f32) * 0.1,
        'w_pool': nrm(ks[12], (DEPTH, POOL_GROUPS, POOL_CH, POOL_CH), f32) * POOL_CH ** -0.5,
        'pool_scale': 1.0 + 0.05 * nrm(ks[13], (DEPTH, POOL_WIDTH), f32),
    }


def reference(x, c, ctx, c_ctx, w_mod, b_mod, norm_g, w_ffn_gate_up, w_ffn_down, w_in, w_out, na_rpb, w_pool, pool_scale):
    for l in range(DEPTH):
        last = l == DEPTH - 1
        mx = [m[:, None, :] for m in mod_vectors(c, w_mod[l], b_mod[l])]
        mc = mod_vectors(c_ctx, w_mod[l], b_mod[l])
        g = norm_g[l]
        x = sandwich_ffn(x, w_ffn_gate_up[l, 0], w_ffn_down[l, 0], g[0], g[1], mx[0], mx[1], mx[2])
        ctx = sandwich_ffn(ctx, w_ffn_gate_up[l, 0], w_ffn_down[l, 0], g[0], g[1], mc[0], mc[1], mc[2])
        hx = modulate(rms_norm(x, g[2]), mx[3], mx[4])
        hc = modulate(rms_norm(ctx, g[2]), mc[3], mc[4])
        out_x, out_c = token_mix(hx, hc, w_in[l], w_out[l], na_rpb[l], w_pool[l], pool_scale[l], not last)
        x = x + mx[5] * rms_norm(out_x, g[3])
        x = sandwich_ffn(x, w_ffn_gate_up[l, 1], w_ffn_down[l, 1], g[4], g[5], mx[6], mx[7], mx[8])
        if not last:
            ctx = ctx + mc[5] * rms_norm(out_c, g[3])
            ctx = sandwich_ffn(ctx, w_ffn_gate_up[l, 1], w_ffn_down[l, 1], g[4], g[5], mc[6], mc[7], mc[8])
    return x
```

```python
import contextlib
import numpy as np
import concourse.bass as bass
import concourse.mybir as mybir
from concourse.bass_utils import run_bass_kernel_spmd

F32 = mybir.dt.float32
BF16 = mybir.dt.bfloat16
ALU = mybir.AluOpType
AF = mybir.ActivationFunctionType

D = 1024
KC = 8
DEPTH = 2
GW = 64
ROWS = 128
WROWS = 48
WTOK = WROWS * GW
NBLK = WROWS // 2
CTX = 256
DFF = 2816
HC = 22
NH = 8
NEG = -30000.0
EPS = 1e-6


class Prog:
    ENG = ("pe", "act", "dve", "pool", "sp")

    def __init__(self, nc, stack, n_dma_sems=24):
        self.nc = nc
        self.stack = stack
        self.eng = {"pe": nc.tensor, "act": nc.scalar, "dve": nc.vector,
                    "pool": nc.gpsimd, "sp": nc.sync}
        self.sem = {}
        self.cnt = {}
        self.epoch = 0
        for e in self.ENG:
            self.sem[e] = stack.enter_context(nc.semaphore(f"s_{e}_0"))
            self.cnt[e] = 0
        self.dsem = [stack.enter_context(nc.semaphore(f"d_{i}")) for i in range(n_dma_sems)]
        self.dcnt = [0] * n_dma_sems
        self.n_sw = n_dma_sems // 3
        self.dnext_sw = 0
        self.dnext_hw = 0
        self.waited = {e: {} for e in self.ENG}
        self.buf = {}
        self.nops = 0

    def _wait(self, e, tok):
        if tok is None:
            return
        if tok[0] == 'e':
            _, src, ep, seq = tok
            if ep != self.epoch:
                return
            if src == e and e == "pe":
                return
            key = ('e', src)
            if self.waited[e].get(key, 0) >= seq:
                return
            self.eng[e].wait_ge(self.sem[src], seq)
            self.waited[e][key] = seq
        else:
            _, idx, val = tok
            key = ('d', idx)
            if self.waited[e].get(key, 0) >= val:
                return
            self.eng[e].wait_ge(self.dsem[idx], val)
            self.waited[e][key] = val

    def _deps(self, e, reads, writes):
        for k in reads:
            st = self.buf.get(k)
            if st:
                self._wait(e, st[0])
        for k in writes:
            st = self.buf.get(k)
            if st:
                self._wait(e, st[0])
                for t in st[1]:
                    self._wait(e, t)

    def _update(self, tok, reads, writes):
        for k in reads:
            st = self.buf.setdefault(k, [None, []])
            st[1].append(tok)
            if len(st[1]) > 48:
                last = {}
                keep = []
                for t in st[1]:
                    if t[0] == 'e':
                        last[t[1]] = t
                    else:
                        keep.append(t)
                st[1] = keep[-40:] + list(last.values())
        for k in writes:
            self.buf[k] = [tok, []]

    def op(self, e, fn, reads=(), writes=()):
        self._deps(e, reads, writes)
        ins = fn(self.eng[e])
        self.cnt[e] += 1
        ins.then_inc(self.sem[e], 1)
        tok = ('e', e, self.epoch, self.cnt[e])
        self._update(tok, reads, writes)
        self.nops += 1
        return tok

    def dma(self, q, out, in_, reads=(), writes=(), **kw):
        self._deps(q, reads, writes)
        if q == "pool":
            idx = self.dnext_sw
            self.dnext_sw = (self.dnext_sw + 1) % self.n_sw
        else:
            idx = self.n_sw + self.dnext_hw
            self.dnext_hw = (self.dnext_hw + 1) % (len(self.dsem) - self.n_sw)
        if self.dcnt[idx] > 0:
            self._wait(q, ('d', idx, self.dcnt[idx]))
        ins = self.eng[q].dma_start(out=out, in_=in_, **kw)
        self.dcnt[idx] += 16
        ins.then_inc(self.dsem[idx], 16)
        tok = ('d', idx, self.dcnt[idx])
        self._update(tok, reads, writes)
        self.nops += 1
        return tok

    def drain(self, e):
        for idx in range(len(self.dsem)):
            if self.dcnt[idx] > 0:
                self._wait(e, ('d', idx, self.dcnt[idx]))
        for src in self.ENG:
            if self.cnt[src] > 0 and src != e:
                self._wait(e, ('e', src, self.epoch, self.cnt[src]))

    def new_epoch(self):
        for e in self.ENG:
            self.drain(e)
        self.epoch += 1
        for e in self.ENG:
            self.sem[e] = self.stack.enter_context(self.nc.semaphore(f"s_{e}_{self.epoch}"))
            self.cnt[e] = 0
        for e in self.ENG:
            self.waited[e] = {k: v for k, v in self.waited[e].items() if k[0] == 'd'}
        self.buf = {}


def build(debug=False, nphase=99):
    nc = bass.Bass("TRN2", target_bir_lowering=False)

    def din(name, shape):
        return nc.dram_tensor(name, list(shape), F32, kind="ExternalInput").ap()

    xT = din("xT", [D, WTOK])
    cT = din("cT", [D, CTX])
    sT = din("sT", [128, 16])
    wmod = din("wmod", [DEPTH, D, 9 * D])
    bm = din("bm", [128, DEPTH * 72])
    ng = din("ng", [128, DEPTH * 48])
    wgu = din("wgu", [DEPTH, 2, D, 2 * DFF])
    wd = din("wd", [DEPTH, 2, DFF, D])
    win = din("win", [DEPTH, D, 2048])
    wout = din("wout", [DEPTH, D, D])
    wpool = din("wpool", [DEPTH, 4, 128, 128])
    psc = din("psc", [128, DEPTH * 4])
    bt = din("bt", [DEPTH, NH, 128, 896])
    ropeC = din("ropeC", [128, WTOK])
    ropeS = din("ropeS", [128, WTOK])
    kmask = din("kmask", [16, WTOK])
    qmask = din("qmask", [16, WTOK])
    icx = din("icx", [128, 4 * 64])
    icc = din("icc", [128, 4 * 256])
    pmi = din("pmi", [128, 384])

    skind = "ExternalOutput" if debug else "Internal"
    XS = [xT] + [nc.dram_tensor(f"X{i}", [D, WTOK], F32, kind=skind).ap() for i in range(1, 6)]
    CS = [cT] + [nc.dram_tensor(f"C{i}", [D, CTX], F32, kind=skind).ap() for i in range(1, 5)]
    outT = nc.dram_tensor("outT", [D, 2048], F32, kind="ExternalOutput").ap()
    qm_bf = nc.dram_tensor("qm_bf", [16, WTOK], BF16, kind="Internal").ap()

    def xview(ap, t0, n):
        return ap.rearrange("(kc p) t -> p kc t", p=128)[:, :, t0:t0 + n]

    with contextlib.ExitStack() as st:
        P = Prog(nc, st)

        uid = [0]

        def SB(stack, name, shape, dt):
            uid[0] += 1
            return stack.enter_context(nc.sbuf_tensor(f"{name}_{uid[0]}", list(shape), dt))

        def PS(stack, name, shape, dt=F32):
            uid[0] += 1
            return stack.enter_context(nc.psum_tensor(f"{name}_{uid[0]}", list(shape), dt))

        ones_bf = SB(st, "ones_bf", [128, 128], BF16)
        pmi_bf = SB(st, "pmi_bf", [128, 384], BF16)
        ng_sb = SB(st, "ng_sb", [128, DEPTH * 48], F32)
        psc_sb = SB(st, "psc_sb", [128, DEPTH * 4], F32)
        eps_sb = SB(st, "eps_sb", [128, 1], F32)
        cs_sb = SB(st, "cs_sb", [128, DEPTH * 2 * 3 * 3 * 8], F32)

        def cs(l, kind, sub, which):
            o = (((l * 2 + kind) * 3 + sub) * 3 + which) * 8
            return cs_sb[:, o:o + 8]

        pm_bf = pmi_bf[:, 0:128]
        ident_bf = pmi_bf[:, 128:256]
        ident8_bf = pmi_bf[:, 256:384]

        P.op("dve", lambda e: e.memset(ones_bf[:], 1.0), writes=["ones"])
        P.op("dve", lambda e: e.memset(eps_sb[:], EPS), writes=["eps"])
        P.dma("pool", pmi_bf[:], pmi, writes=["pmi"])
        P.dma("sp", ng_sb[:], ng, writes=["ng"])
        P.dma("sp", psc_sb[:], psc, writes=["psc"])
        P.dma("pool", qm_bf, qmask)

        with contextlib.ExitStack() as ph:
            s_sb = SB(ph, "s_sb", [128, 16], F32)
            s_bf = SB(ph, "s_bf", [128, 16], BF16)
            bm_sb = SB(ph, "bm_sb", [128, DEPTH * 72], F32)
            mod_sb = SB(ph, "mod_sb", [128, DEPTH * 2 * 72], F32)
            tmp8 = SB(ph, "tmp8", [128, 8], F32)
            wm = [SB(ph, f"wm{i}", [128, KC, 1024], BF16) for i in range(2)]
            mps = PS(ph, "mps", [128, 512])
            P.dma("sp", s_sb[:], sT, writes=["s_sb"])
            P.dma("sp", bm_sb[:], bm, writes=["bm"])
            P.op("act", lambda e: e.activation(s_bf[:], s_sb[:], AF.Silu), reads=["s_sb"], writes=["s_bf"])
            s3 = s_bf[:].rearrange("p (k t) -> p k t", t=2)
            for l in range(DEPTH):
                for gi in range(9):
                    w = wm[(l * 9 + gi) % 2]
                    wk = f"wm{(l * 9 + gi) % 2}"
                    src = wmod[l].rearrange("(kc p) n -> p kc n", p=128)[:, :, gi * 1024:(gi + 1) * 1024]
                    P.dma("pool", w[:, 0:4, :], src[:, 0:4, :], writes=[wk + "a"])
                    P.dma("pool", w[:, 4:8, :], src[:, 4:8, :], writes=[wk + "b"])
                    for oc in range(8):
                        col = (gi * 8 + oc) * 2
                        for kc in range(KC):
                            P.op("pe", lambda e: e.matmul(mps[:, col:col + 2], w[:, kc, oc * 128:(oc + 1) * 128],
                                                          s3[:, kc, :], start=(kc == 0), stop=(kc == KC - 1)),
                                 reads=[wk + "a", wk + "b", "s_bf"], writes=["mps"])
                m3 = mps[:, 0:144].rearrange("p (m t) -> p m t", t=2)
                for kind in range(2):
                    o = (l * 2 + kind) * 72
                    P.op("dve", lambda e: e.tensor_tensor(mod_sb[:, o:o + 72], m3[:, :, kind],
                                                          bm_sb[:, l * 72:(l + 1) * 72], ALU.add),
                         reads=["bm"], writes=[f"mod{l}{kind}", "mps"])
                    for sub in range(3):
                        pre, post = [(0, 1), (2, 3), (4, 5)][sub]
                        coef = [0.5, 1.0, 0.5][sub]
                        sh = mod_sb[:, o + (3 * sub) * 8: o + (3 * sub) * 8 + 8]
                        sc_ = mod_sb[:, o + (3 * sub + 1) * 8: o + (3 * sub + 1) * 8 + 8]
                        gt = mod_sb[:, o + (3 * sub + 2) * 8: o + (3 * sub + 2) * 8 + 8]
                        gpre = ng_sb[:, l * 48 + pre * 8: l * 48 + pre * 8 + 8]
                        gpost = ng_sb[:, l * 48 + post * 8: l * 48 + post * 8 + 8]
                        P.op("dve", lambda e: e.tensor_scalar(tmp8[:], sc_, 1.0, None, ALU.add),
                             reads=[f"mod{l}{kind}"], writes=["tmp8"])
                        P.op("dve", lambda e: e.tensor_tensor(cs(l, kind, sub, 0), tmp8[:], gpre, ALU.mult),
                             reads=["tmp8", "ng"], writes=["cs"])
                        P.op("dve", lambda e: e.tensor_copy(cs(l, kind, sub, 1), sh),
                             reads=[f"mod{l}{kind}"], writes=["cs"])
                        P.op("dve", lambda e: e.scalar_tensor_tensor(cs(l, kind, sub, 2), gt, coef, gpost,
                                                                     ALU.mult, ALU.mult),
                             reads=[f"mod{l}{kind}", "ng"], writes=["cs"])
            P.new_epoch()

        def rms_stats(src3, n, srckeys, bufs):
            for kc in range(KC):
                sq = bufs["sq"][kc % 2]
                sk = f"sq{kc % 2}"
                P.op("act", lambda e: e.activation(sq[:, 0:n], src3[:, kc, :], AF.Square),
                     reads=srckeys, writes=[sk])
                P.op("pe", lambda e: e.matmul(bufs["stat_ps"][:, 0:n], ones_bf[:], sq[:, 0:n],
                                              start=(kc == 0), stop=(kc == KC - 1)),
                     reads=[sk, "ones"], writes=[bufs["stat_key"]])
            P.op("act", lambda e: e.activation(bufs["rt"][:, 0:n], bufs["stat_ps"][:, 0:n], AF.Sqrt,
                                               bias=eps_sb[:, 0:1], scale=1.0 / D),
                 reads=["eps"], writes=["rt", bufs["stat_key"]])
            P.op("dve", lambda e: e.reciprocal(bufs["rstd"][:, 0:n], bufs["rt"][:, 0:n]),
                 reads=["rt"], writes=["rstd"])

        def rms_stats_b(src3, n, srckeys, bufs, sq_fn, nslots, sqkeys):
            for g0 in range(0, KC, nslots):
                for j in range(nslots):
                    P.op("act", lambda e: e.activation(sq_fn(j), src3[:, g0 + j, :], AF.Square),
                         reads=srckeys, writes=sqkeys)
                for j in range(nslots):
                    kc = g0 + j
                    P.op("pe", lambda e: e.matmul(bufs["stat_ps"][:, 0:n], ones_bf[:], sq_fn(j),
                                                  start=(kc == 0), stop=(kc == KC - 1)),
                         reads=list(sqkeys) + ["ones"], writes=[bufs["stat_key"]])
            P.op("act", lambda e: e.activation(bufs["rt"][:, 0:n], bufs["stat_ps"][:, 0:n], AF.Sqrt,
                                               bias=eps_sb[:, 0:1], scale=1.0 / D),
                 reads=["eps"], writes=["rt", bufs["stat_key"]])
            P.op("dve", lambda e: e.reciprocal(bufs["rstd"][:, 0:n], bufs["rt"][:, 0:n]),
                 reads=["rt"], writes=["rstd"])

        def norm_mod(src3, n, srckeys, bufs, A, B, dst3, dstkeys, batched=False):
            if batched:
                rms_stats_b(src3, n, srckeys, bufs, lambda j: dst3[:, j, :], KC, dstkeys)
            else:
                rms_stats(src3, n, srckeys, bufs)
            for kc in range(KC):
                tb = bufs["nt"][kc % 2]
                tk = f"nt{kc % 2}"
                P.op("dve", lambda e: e.scalar_tensor_tensor(tb[:, 0:n], src3[:, kc, :], A[:, kc:kc + 1],
                                                             bufs["rstd"][:, 0:n], ALU.mult, ALU.mult),
                     reads=list(srckeys) + ["rstd", "cs"], writes=[tk])
                P.op("act", lambda e: e.activation(dst3[:, kc, :], tb[:, 0:n], AF.Identity,
                                                   bias=B[:, kc:kc + 1], scale=1.0),
                     reads=[tk, "cs"], writes=dstkeys)

        def post_res(y3, x3, n, ykeys, xkeys, bufs, C, dst_dram3, q="sp", sqp=None):
            if sqp is not None:
                rms_stats_b(y3, n, ykeys, bufs, lambda j: sqp[:, j, 0:n], 4, ["sqp"])
            else:
                rms_stats(y3, n, ykeys, bufs)
            for kc in range(KC):
                tb = bufs["nt"][kc % 2]
                tk = f"nt{kc % 2}"
                P.op("dve", lambda e: e.scalar_tensor_tensor(tb[:, 0:n], y3[:, kc, :], C[:, kc:kc + 1],
                                                             bufs["rstd"][:, 0:n], ALU.mult, ALU.mult),
                     reads=list(ykeys) + ["rstd", "cs"], writes=[tk])
                P.op("pool", lambda e: e.tensor_tensor(x3[:, kc, :], x3[:, kc, :], tb[:, 0:n], ALU.add),
                     reads=[tk] + list(xkeys), writes=xkeys)
            P.dma(q, dst_dram3, x3, reads=xkeys)

        def ffn_phase(l, f, passes):
            sub = 0 if f == 0 else 2
            with contextlib.ExitStack() as ph:
                TP = 1280
                h_sb = SB(ph, "h_sb", [128, KC, TP], BF16)
                act_sb = SB(ph, "act_sb", [128, HC, TP], BF16)
                y_sb = SB(ph, "y_sb", [128, KC, TP], F32)
                xt = [SB(ph, f"xt{i}", [128, KC, 512], F32) for i in range(2)]
                NG = 2
                gu = [SB(ph, f"gu{i}", [128, 2, KC, 256], BF16) for i in range(NG)]
                dw = [SB(ph, f"dw{i}", [128, HC, 256], BF16) for i in range(2)]
                sg = [SB(ph, f"sg{i}", [128, 512], F32) for i in range(2)]
                sqp = SB(ph, "sqp", [128, 4, 512], BF16)
                bufs = {
                    "nt": [SB(ph, f"nt{i}", [128, 512], F32) for i in range(2)],
                    "rt": SB(ph, "rt", [128, 512], F32),
                    "rstd": SB(ph, "rstd", [128, 512], F32),
                    "stat_ps": PS(ph, "stat_ps", [128, 512]),
                    "stat_key": "stat_ps",
                }
                gps = [PS(ph, f"gps{i}", [128, 512]) for i in range(2)]
                ups = [PS(ph, f"ups{i}", [128, 512]) for i in range(2)]
                dps = [PS(ph, f"dps{i}", [128, 512]) for i in range(2)]
                wgu3 = wgu[l, f].rearrange("(kc p) n -> p kc n", p=128)
                wd3 = wd[l, f].rearrange("(kc p) n -> p kc n", p=128)
                st_ = dict(g=0, d=0, pc=0, xc=0)

                def offs_of(tiles):
                    offs, o = [], 0
                    for t in tiles:
                        offs.append(o)
                        o += t["n"]
                    assert o <= TP
                    return offs

                def load_gu(j, slot):
                    P.dma("pool", gu[slot][:, 0], wgu3[:, :, j * 256:(j + 1) * 256], writes=[f"gu{slot}g"])
                    P.dma("pool", gu[slot][:, 1], wgu3[:, :, DFF + j * 256:DFF + (j + 1) * 256], writes=[f"gu{slot}u"])

                def load_dw(op_, slot):
                    P.dma("pool", dw[slot][:, 0:11, :], wd3[:, 0:11, op_ * 256:(op_ + 1) * 256], writes=[f"dw{slot}a"])
                    P.dma("pool", dw[slot][:, 11:22, :], wd3[:, 11:22, op_ * 256:(op_ + 1) * 256], writes=[f"dw{slot}b"])

                def stat_tail(n):
                    P.op("act", lambda e: e.activation(bufs["rt"][:, 0:n], bufs["stat_ps"][:, 0:n], AF.Sqrt,
                                                       bias=eps_sb[:, 0:1], scale=1.0 / D),
                         reads=["eps"], writes=["rt", "stat_ps"])
                    P.op("dve", lambda e: e.reciprocal(bufs["rstd"][:, 0:n], bufs["rt"][:, 0:n]),
                         reads=["rt"], writes=["rstd"])

                def norm_units(tiles):
                    offs = offs_of(tiles)
                    units = []
                    for ti, t in enumerate(tiles):
                        n = t["n"]
                        xi = st_["xc"] % 2
                        st_["xc"] += 1
                        x3 = xt[xi][:, :, 0:n]
                        xk = f"xt{xi}"
                        h3 = h_sb[:, :, offs[ti]:offs[ti] + n]
                        hk = f"h{ti}"
                        A_, B_ = cs(l, t["kind"], sub, 0), cs(l, t["kind"], sub, 1)

                        def u0(x3=x3, xk=xk, t=t, n=n, h3=h3, hk=hk):
                            P.dma("sp", x3, t["src"], writes=[xk])
                            for kc in range(KC):
                                P.op("act", lambda e: e.activation(h3[:, kc, :], x3[:, kc, :], AF.Square), reads=[xk], writes=[hk])

                        def u1(n=n, h3=h3, hk=hk):
                            for kc in range(KC):
                                P.op("pe", lambda e: e.matmul(bufs["stat_ps"][:, 0:n], ones_bf[:], h3[:, kc, :],
                                                              start=(kc == 0), stop=(kc == KC - 1)),
                                     reads=[hk, "ones"], writes=["stat_ps"])

                        def u2(x3=x3, xk=xk, n=n, h3=h3, hk=hk, A_=A_, B_=B_):
                            stat_tail(n)
                            for kc in range(KC):
                                tb = bufs["nt"][kc % 2]
                                tk = f"nt{kc % 2}"
                                P.op("dve", lambda e: e.scalar_tensor_tensor(tb[:, 0:n], x3[:, kc, :], A_[:, kc:kc + 1],
                                                                             bufs["rstd"][:, 0:n], ALU.mult, ALU.mult),
                                     reads=[xk, "rstd", "cs"], writes=[tk])
                                P.op("act", lambda e: e.activation(h3[:, kc, :], tb[:, 0:n], AF.Identity,
                                                                   bias=B_[:, kc:kc + 1], scale=1.0),
                                     reads=[tk, "cs"], writes=[hk])
                        units += [u0, u1, u2]
                    return units

                def post_units(tiles):
                    offs = offs_of(tiles)
                    units = []
                    for ti, t in enumerate(tiles):
                        n = t["n"]
                        xi = st_["xc"] % 2
                        st_["xc"] += 1
                        x3 = xt[xi][:, :, 0:n]
                        xk = f"xt{xi}"
                        y3 = y_sb[:, :, offs[ti]:offs[ti] + n]
                        yk = f"y{ti}"
                        C_ = cs(l, t["kind"], sub, 2)

                        def sq(g0, y3=y3, yk=yk, n=n):
                            for j in range(4):
                                P.op("act", lambda e: e.activation(sqp[:, j, 0:n], y3[:, g0 + j, :], AF.Square), reads=[yk], writes=["sqp"])

                        def mm(g0, n=n):
                            for j in range(4):
                                kc = g0 + j
                                P.op("pe", lambda e: e.matmul(bufs["stat_ps"][:, 0:n], ones_bf[:], sqp[:, j, 0:n],
                                                              start=(kc == 0), stop=(kc == KC - 1)),
                                     reads=["sqp", "ones"], writes=["stat_ps"])

                        def u0(x3=x3, xk=xk, t=t, sq=sq):
                            P.dma("sp", x3, t["src"], writes=[xk])
                            sq(0)

                        def u1(sq=sq, mm=mm):
                            mm(0)
                            sq(4)

                        def u2(mm=mm, n=n):
                            mm(4)
                            stat_tail(n)

                        def u3(x3=x3, xk=xk, y3=y3, yk=yk, n=n, C_=C_, t=t):
                            for kc in range(KC):
                                tb = bufs["nt"][kc % 2]
                                tk = f"nt{kc % 2}"
                                P.op("dve", lambda e: e.scalar_tensor_tensor(tb[:, 0:n], y3[:, kc, :], C_[:, kc:kc + 1],
                                                                             bufs["rstd"][:, 0:n], ALU.mult, ALU.mult),
                                     reads=[yk, "rstd", "cs"], writes=[tk])
                                P.op("dve", lambda e: e.tensor_tensor(x3[:, kc, :], x3[:, kc, :], tb[:, 0:n], ALU.add),
                                     reads=[tk, xk], writes=[xk])
                            P.dma("sp", t["dst"], x3, reads=[xk])
                        units += [u0, u1, u2, u3]
                    return units

                def emit_norm(tiles):
                    for u in norm_units(tiles):
                        u()

                def emit_post(tiles):
                    for u in post_units(tiles):
                        u()

                def emit_gu(tiles, inject, has_next):
                    offs = offs_of(tiles)
                    g0 = st_["g"]
                    for j in range(11):
                        slot = (g0 + j) % NG
                        if j + 1 < 11:
                            load_gu(j + 1, (g0 + j + 1) % NG)
                        elif has_next:
                            load_gu(0, (g0 + 11) % NG)
                        if j == 8:
                            load_dw(0, st_["d"] % 2)
                        if j == 10:
                            load_dw(1, (st_["d"] + 1) % 2)
                        for ti, t in enumerate(tiles):
                            n = t["n"]
                            hs = h_sb[:, :, offs[ti]:offs[ti] + n]
                            for s_ in range(2):
                                b = st_["pc"] % 2
                                st_["pc"] += 1
                                for kc in range(KC):
                                    P.op("pe", lambda e: e.matmul(gps[b][:, 0:n], gu[slot][:, 0, kc, s_ * 128:(s_ + 1) * 128],
                                                                  hs[:, kc, :], start=(kc == 0), stop=(kc == KC - 1)),
                                         reads=[f"gu{slot}g", f"h{ti}"], writes=[f"gps{b}"])
                                for kc in range(KC):
                                    P.op("pe", lambda e: e.matmul(ups[b][:, 0:n], gu[slot][:, 1, kc, s_ * 128:(s_ + 1) * 128],
                                                                  hs[:, kc, :], start=(kc == 0), stop=(kc == KC - 1)),
                                         reads=[f"gu{slot}u", f"h{ti}"], writes=[f"ups{b}"])
                                P.op("act", lambda e: e.activation(sg[b][:, 0:n], gps[b][:, 0:n], AF.Silu),
                                     reads=[], writes=[f"sg{b}", f"gps{b}"])
                                hi = j * 2 + s_
                                P.op("dve", lambda e: e.tensor_tensor(act_sb[:, hi, offs[ti]:offs[ti] + n],
                                                                      sg[b][:, 0:n], ups[b][:, 0:n], ALU.mult),
                                     reads=[f"sg{b}"], writes=[f"act{ti}", f"ups{b}"])
                            if j >= 1 and inject:
                                inject.pop(0)()
                    while inject:
                        inject.pop(0)()
                    st_["g"] += 11

                def emit_down(tiles, inject):
                    offs = offs_of(tiles)
                    d0 = st_["d"]
                    for op_ in range(4):
                        slot = (d0 + op_) % 2
                        for ti, t in enumerate(tiles):
                            n = t["n"]
                            for s_ in range(2):
                                b = st_["pc"] % 2
                                st_["pc"] += 1
                                for hc in range(HC):
                                    P.op("pe", lambda e: e.matmul(dps[b][:, 0:n], dw[slot][:, hc, s_ * 128:(s_ + 1) * 128],
                                                                  act_sb[:, hc, offs[ti]:offs[ti] + n],
                                                                  start=(hc == 0), stop=(hc == HC - 1)),
                                         reads=[f"dw{slot}a", f"dw{slot}b", f"act{ti}"], writes=[f"dps{b}"])
                                oc = op_ * 2 + s_
                                if (oc % 2) == 0:
                                    P.op("act", lambda e: e.activation(y_sb[:, oc, offs[ti]:offs[ti] + n], dps[b][:, 0:n],
                                                                       AF.Identity),
                                         reads=[], writes=[f"y{ti}", f"dps{b}"])
                                else:
                                    P.op("dve", lambda e: e.tensor_copy(y_sb[:, oc, offs[ti]:offs[ti] + n], dps[b][:, 0:n]),
                                         reads=[], writes=[f"y{ti}", f"dps{b}"])
                            if inject:
                                inject.pop(0)()
                        if op_ + 2 < 4:
                            load_dw(op_ + 2, (d0 + op_ + 2) % 2)
                    while inject:
                        inject.pop(0)()
                    st_["d"] += 4

                load_gu(0, st_["g"] % NG)
                emit_norm(passes[0])
                for p, tiles in enumerate(passes):
                    inj = post_units(passes[p - 1]) if p > 0 else []
                    emit_gu(tiles, inj, p + 1 < len(passes))
                    ninj = norm_units(passes[p + 1]) if p + 1 < len(passes) else []
                    emit_down(tiles, ninj)
                emit_post(passes[-1])
                P.new_epoch()

        def mix_phase(l, srcX, dstX, kv_t0, kv_ntiles, q_blk0, q_nblk, srcC, dstC):
            with contextlib.ExitStack() as ph:
                KTa = SB(ph, "KTa", [80, NH, WTOK], BF16)
                KCa = SB(ph, "KCa", [80, NH, CTX], BF16)
                V_sb = SB(ph, "V_sb", [128, NBLK, NH, 65], BF16)
                VC_sb = SB(ph, "VC_sb", [128, 2, NH, 65], BF16)
                TTx = SB(ph, "TTx", [128, NH, 896], BF16)
                wpl = SB(ph, "wpl", [128, 4, 128], BF16)
                Bk = [PS(ph, f"B{i}", [128, 512]) for i in range(8)]
                m1s = contextlib.ExitStack()

                P.op("dve", lambda e: e.memset(V_sb[:, :, :, 64:65], 1.0), writes=["V"])
                P.op("dve", lambda e: e.memset(VC_sb[:, :, :, 64:65], 1.0), writes=["VC"])
                P.op("pool", lambda e: e.memset(KCa[64:80, :, :], 0.0), writes=["KCa"])
                win3 = win[l].rearrange("(kc p) n -> p kc n", p=128)
                wkv = SB(m1s, "wkv", [128, KC, 1024], BF16)
                P.dma("pool", wkv[:, 0:4, :], win3[:, 0:4, 512:1536], writes=["wkva"])
                P.dma("pool", wkv[:, 4:8, :], win3[:, 4:8, 512:1536], writes=["wkvb"])
                for h in range(NH):
                    P.dma("pool", KTa[64:80, h, :], kmask, writes=[f"KTm{h}"])
                P.dma("pool", wpl[:], wpool[l].rearrange("g c d -> c g d"), writes=["wpl"])
                btl = bt[l].rearrange("h p c -> p h c")
                P.dma("pool", TTx[:, 0:4, :], btl[:, 0:4, :], writes=["TTx"])
                P.dma("pool", TTx[:, 4:8, :], btl[:, 4:8, :], writes=["TTxb"])

                with contextlib.ExitStack() as p1:
                    R_ = []
                    for p_ in range(2):
                        R_.append(dict(
                            xt=SB(p1, f"xt{p_}", [128, KC, 512], F32), hm=SB(p1, f"hm{p_}", [128, KC, 512], BF16),
                            rc=SB(p1, f"rc{p_}", [128, 512], F32), rs=SB(p1, f"rs{p_}", [128, 512], F32),
                            qb=SB(p1, f"qb{p_}", [128, 512], BF16), t1=SB(p1, f"t1{p_}", [128, 512], F32),
                            t2=SB(p1, f"t2{p_}", [128, 512], F32),
                            nt=[SB(p1, f"nt{p_}{i}", [128, 512], F32) for i in range(2)],
                            rt=SB(p1, f"rt{p_}", [128, 512], F32), rstd=SB(p1, f"rstd{p_}", [128, 512], F32),
                            K=Bk[3 * p_], R=Bk[3 * p_ + 1], V=Bk[3 * p_ + 2],
                            Kk=f"B{3 * p_}", Rk=f"B{3 * p_ + 1}", Vk=f"B{3 * p_ + 2}", p=str(p_)))
                    kvtiles = [("x", kv_t0 + 512 * i, 512) for i in range(kv_ntiles)] + [("c", 0, CTX)]

                    def m1_units(ti, r):
                        kd, t0, n = kvtiles[ti]
                        kind = 0 if kd == "x" else 1
                        pz = r["p"]
                        x3 = r["xt"][:, :, 0:n]
                        hm_ = r["hm"]
                        A_, B_ = cs(l, kind, 1, 0), cs(l, kind, 1, 1)
                        units = []

                        def u_load():
                            src = xview(srcX, t0, n) if kd == "x" else xview(srcC, 0, n)
                            P.dma("sp", x3, src, writes=["xt" + pz])
                            if kd == "x":
                                P.dma("sp", r["rc"][:, 0:n], ropeC[:, t0:t0 + n], writes=["rc" + pz])
                                P.dma("sp", r["rs"][:, 0:n], ropeS[:, t0:t0 + n], writes=["rs" + pz])
                        units.append(u_load)

                        def u_sq():
                            for kc in range(KC):
                                P.op("act", lambda e: e.activation(hm_[:, kc, 0:n], x3[:, kc, :], AF.Square),
                                     reads=["xt" + pz], writes=["hm" + pz])

                        def u_st():
                            for kc in range(KC):
                                P.op("pe", lambda e: e.matmul(r["R"][:, 0:n], ones_bf[:], hm_[:, kc, 0:n],
                                                              start=(kc == 0), stop=(kc == KC - 1)),
                                     reads=["hm" + pz, "ones"], writes=[r["Rk"]])

                        def u_rs():
                            P.op("act", lambda e: e.activation(r["rt"][:, 0:n], r["R"][:, 0:n], AF.Sqrt,
                                                               bias=eps_sb[:, 0:1], scale=1.0 / D),
                                 reads=["eps"], writes=["rt" + pz, r["Rk"]])
                            P.op("dve", lambda e: e.reciprocal(r["rstd"][:, 0:n], r["rt"][:, 0:n]),
                                 reads=["rt" + pz], writes=["rstd" + pz])
                        units += [u_sq, u_st, u_rs]
                        for g0 in (0, 4):
                            def u_ap(g0=g0):
                                for kc in range(g0, g0 + 4):
                                    tb = r["nt"][kc % 2]
                                    tk = f"nt{pz}{kc % 2}"
                                    P.op("dve", lambda e: e.scalar_tensor_tensor(tb[:, 0:n], x3[:, kc, :], A_[:, kc:kc + 1],
                                                                                 r["rstd"][:, 0:n], ALU.mult, ALU.mult),
                                         reads=["xt" + pz, "rstd" + pz, "cs"], writes=[tk])
                                    P.op("act", lambda e: e.activation(hm_[:, kc, 0:n], tb[:, 0:n], AF.Identity,
                                                                       bias=B_[:, kc:kc + 1], scale=1.0),
                                         reads=[tk, "cs"], writes=["hm" + pz])
                            units.append(u_ap)
                        for c2 in range(4):
                            def u_ka(c2=c2):
                                for kc in range(KC):
                                    P.op("pe", lambda e: e.matmul(r["K"][:, 0:n], wkv[:, kc, c2 * 128:(c2 + 1) * 128], hm_[:, kc, 0:n],
                                                                  start=(kc == 0), stop=(kc == KC - 1)),
                                         reads=["wkva", "wkvb", "hm" + pz], writes=[r["Kk"]])
                            if kd == "x":
                                def u_kb(c2=c2):
                                    P.op("act", lambda e: e.activation(r["qb"][:, 0:n], r["K"][:, 0:n], AF.Identity),
                                         reads=[], writes=["qb" + pz, r["Kk"]])
                                    P.op("dve", lambda e: e.tensor_tensor(r["t1"][:, 0:n], r["K"][:, 0:n], r["rc"][:, 0:n], ALU.mult),
                                         reads=["rc" + pz], writes=["t1" + pz, r["Kk"]])

                                def u_kc(c2=c2):
                                    P.op("pe", lambda e: e.matmul(r["R"][:, 0:n], pm_bf, r["qb"][:, 0:n], start=True, stop=True),
                                         reads=["qb" + pz, "pmi"], writes=[r["Rk"]])

                                def u_kd(c2=c2):
                                    P.op("dve", lambda e: e.tensor_tensor(r["t2"][:, 0:n], r["R"][:, 0:n], r["rs"][:, 0:n], ALU.mult),
                                         reads=["rs" + pz], writes=["t2" + pz, r["Rk"]])
                                    for j in range(2):
                                        P.op("pool", lambda e: e.tensor_tensor(KTa[0:64, 2 * c2 + j, t0:t0 + n],
                                                                               r["t1"][64 * j:64 * j + 64, 0:n],
                                                                               r["t2"][64 * j:64 * j + 64, 0:n], ALU.add),
                                             reads=["t1" + pz, "t2" + pz], writes=[f"KT{c2}"])
                                units += [u_ka, u_kb, u_kc, u_kd]
                            else:
                                def u_kb(c2=c2):
                                    P.op("act", lambda e: e.activation(KCa[0:64, 2 * c2, 0:n], r["K"][0:64, 0:n], AF.Identity),
                                         reads=[], writes=["KCa", r["Kk"]])
                                    P.op("dve", lambda e: e.tensor_copy(KCa[0:64, 2 * c2 + 1, 0:n], r["K"][64:128, 0:n]),
                                         reads=[], writes=["KCa", r["Kk"]])
                                units += [u_ka, u_kb]
                        for bi in range(n // 128):
                            def u_va(bi=bi):
                                for kc in range(KC):
                                    P.op("pe", lambda e: e.matmul(r["V"][:], hm_[:, kc, bi * 128:(bi + 1) * 128], wkv[:, kc, 512:1024],
                                                                  start=(kc == 0), stop=(kc == KC - 1)),
                                         reads=["wkva", "wkvb", "hm" + pz], writes=[r["Vk"]])

                            def u_vb(bi=bi):
                                v3 = r["V"][:].rearrange("p (h d) -> p h d", d=64)
                                if kd == "x":
                                    gb = t0 // 128 + bi
                                    P.op("act", lambda e: e.activation(V_sb[:, gb, :, 0:64], v3, AF.Identity),
                                         reads=[], writes=["V", r["Vk"]])
                                else:
                                    P.op("act", lambda e: e.activation(VC_sb[:, bi, :, 0:64], v3, AF.Identity),
                                         reads=[], writes=["VC", r["Vk"]])
                            units += [u_va, u_vb]
                        return units

                    chains = [[], []]
                    per_tile = [m1_units(ti, R_[ti % 2]) for ti in range(len(kvtiles))]
                    for ti in range(len(kvtiles)):
                        ul = per_tile[ti]
                        if ti + 2 < len(kvtiles):
                            nxt = per_tile[ti + 2].pop(0)
                            ul.insert(len(ul) - 8, nxt)
                        chains[ti % 2] += ul
                    chains[1] = [(lambda: None) for _ in range(12)] + chains[1]
                    k_ = 0
                    while k_ < len(chains[0]) or k_ < len(chains[1]):
                        for c_ in range(2):
                            if k_ < len(chains[c_]):
                                chains[c_][k_]()
                        k_ += 1
                    P.new_epoch()
                m1s.close()

                with contextlib.ExitStack() as p2:
                    W2 = 256
                    wqu = SB(p2, "wqu", [128, KC, 1024], BF16)
                    wo = SB(p2, "wo", [128, KC, 1024], BF16)
                    xt2 = [SB(p2, f"xq{i}", [128, KC, W2], F32) for i in range(2)]
                    hm2 = SB(p2, "hm2", [128, KC, W2], BF16)
                    QTa = [SB(p2, f"QTa{i}", [80, NH, W2], BF16) for i in range(2)]
                    mixT = [SB(p2, f"mixT{i}", [128, KC, W2], BF16) for i in range(3)]
                    y_sb = SB(p2, "y_sb", [128, KC, W2], F32)
                    pA = SB(p2, "pA", [128, 384], F32)
                    pB = SB(p2, "pB", [128, 384], F32)
                    pD = SB(p2, "pD", [128, 256], F32)
                    dT = SB(p2, "dT", [128, 256], BF16)
                    ic_x = SB(p2, "ic_x", [128, 4 * 64], F32)
                    ic_c = SB(p2, "ic_c", [128, 4 * 256], F32)
                    Pb = [SB(p2, f"Pb{i}", [128, 1024], BF16) for i in range(3)]
                    mixA = SB(p2, "mixA", [128, 512], BF16)
                    rec = SB(p2, "rec", [128, 8], F32)
                    rc2 = SB(p2, "rc2", [128, W2], F32)
                    rs2 = SB(p2, "rs2", [128, W2], F32)
                    qb2 = SB(p2, "qb2", [128, W2], BF16)
                    t12 = SB(p2, "t12", [128, W2], F32)
                    t22 = SB(p2, "t22", [128, W2], F32)
                    bufs2 = {
                        "sq": [SB(p2, f"sq2{i}", [128, W2], BF16) for i in range(2)],
                        "nt": [SB(p2, f"nt2{i}", [128, W2], F32) for i in range(2)],
                        "rt": SB(p2, "rt2", [128, W2], F32),
                        "rstd": SB(p2, "rstd2", [128, W2], F32),
                        "stat_ps": Bk[7][:, 256:512],
                        "stat_key": "B7",
                    }
                    Oa, Ob = Bk[4], Bk[5]
                    trpA = Bk[4][:, 384:512].bitcast(BF16)
                    trpB = Bk[5][:, 384:512].bitcast(BF16)
                    pjA = Bk[6]
                    pjB = Bk[7][:, 0:256]
                    P.dma("pool", wqu[:, :, 0:512], win3[:, :, 0:512], writes=["wq"])
                    P.dma("pool", wqu[:, :, 512:1024], win3[:, :, 1536:2048], writes=["wu"])
                    wo3 = wout[l].rearrange("(kc p) n -> p kc n", p=128)
                    P.dma("pool", wo[:, 0:4, :], wo3[:, 0:4, :], writes=["woa"])
                    P.dma("pool", wo[:, 4:8, :], wo3[:, 4:8, :], writes=["wob"])
                    P.dma("sp", ic_x[:], icx, writes=["icx"])
                    P.dma("sp", ic_c[:], icc, writes=["icc"])

                    qtiles = [("x", (q_blk0 + 2 * i) * 128, 256) for i in range(q_nblk // 2)]
                    if dstC is not None:
                        qtiles.append(("c", 0, CTX))
                    NT = len(qtiles)
                    kb0 = kv_t0 // 128
                    kb1 = kb0 + 4 * kv_ntiles
                    scnt = [0]
                    sbs = {}

                    def rope2(n, dst_fn, dkeys):
                        P.op("act", lambda e: e.activation(qb2[:, 0:n], pjA[:, 0:n], AF.Identity), reads=[], writes=["qb2", "B6"])
                        P.op("pe", lambda e: e.matmul(pjB[:, 0:n], pm_bf, qb2[:, 0:n], start=True, stop=True),
                             reads=["qb2", "pmi"], writes=["B7"])
                        P.op("dve", lambda e: e.tensor_tensor(t12[:, 0:n], pjA[:, 0:n], rc2[:, 0:n], ALU.mult),
                             reads=["rc2"], writes=["t12", "B6"])
                        P.op("dve", lambda e: e.tensor_tensor(t22[:, 0:n], pjB[:, 0:n], rs2[:, 0:n], ALU.mult),
                             reads=["rs2"], writes=["t22", "B7"])
                        P.op("pool", lambda e: e.tensor_tensor(dst_fn(0), t12[0:64, 0:n], t22[0:64, 0:n], ALU.add),
                             reads=["t12", "t22"], writes=dkeys)
                        P.op("pool", lambda e: e.tensor_tensor(dst_fn(1), t12[64:128, 0:n], t22[64:128, 0:n], ALU.add),
                             reads=["t12", "t22"], writes=dkeys)

                    def plain2(n, dst_fn, dkeys):
                        P.op("act", lambda e: e.activation(dst_fn(0), pjA[0:64, 0:n], AF.Identity), reads=[], writes=list(dkeys) + ["B6"])
                        P.op("dve", lambda e: e.tensor_copy(dst_fn(1), pjA[64:128, 0:n]), reads=[], writes=list(dkeys) + ["B6"])

                    xP, xT = xt2[0], xt2[1]
                    bufsT = {"nt": [SB(p2, f"ntT{i}", [128, W2], F32) for i in range(2)],
                             "rt": SB(p2, "rtT", [128, W2], F32), "rstd": SB(p2, "rstdT", [128, W2], F32)}
                    pjP = Bk[7][:, 0:256]
                    pjQ = Bk[7][:, 256:512]
                    pjT = Bk[6][:, 0:256]
                    stT = Bk[6][:, 0:256]
                    pjU = Bk[6][:, 256:512]

                    def stats_units(src3, n, srckeys, B, sq_fn, nslots, sqkeys, stat_ap, stat_key):
                        units = []
                        for g0 in range(0, KC, nslots):
                            def ua(g0=g0):
                                for j in range(nslots):
                                    P.op("act", lambda e: e.activation(sq_fn(j), src3[:, g0 + j, :], AF.Square),
                                         reads=srckeys, writes=sqkeys)

                            def ub(g0=g0):
                                for j in range(nslots):
                                    kc = g0 + j
                                    P.op("pe", lambda e: e.matmul(stat_ap[:, 0:n], ones_bf[:], sq_fn(j),
                                                                  start=(kc == 0), stop=(kc == KC - 1)),
                                         reads=list(sqkeys) + ["ones"], writes=[stat_key])
                            units += [ua, ub]

                        def uc():
                            P.op("act", lambda e: e.activation(B["rt"][:, 0:n], stat_ap[:, 0:n], AF.Ln,
                                                               bias=eps_sb[:, 0:1], scale=1.0 / D),
                                 reads=["eps"], writes=[B["k"] + "rt", stat_key])
                            P.op("act", lambda e: e.activation(B["rstd"][:, 0:n], B["rt"][:, 0:n], AF.Exp, scale=-0.5),
                                 reads=[B["k"] + "rt"], writes=[B["k"] + "rstd"])
                        units.append(uc)
                        return units

                    bufs2["k"] = "P"
                    bufsT["k"] = "T"

                    def prep_units(i):
                        kd, t0, n = qtiles[i]
                        kind = 0 if kd == "x" else 1
                        x3 = xP[:, :, 0:n]
                        Q, qk_, qmk = QTa[i % 2], f"QT{i % 2}", f"QTm{i % 2}"
                        MT, mkp = mixT[i % 3], f"mixT{i % 3}p"
                        A_, B_ = cs(l, kind, 1, 0), cs(l, kind, 1, 1)
                        units = []

                        def u_load():
                            src = xview(srcX, t0, n) if kd == "x" else xview(srcC, 0, n)
                            P.dma("sp", x3, src, writes=["xP"])
                            if kd == "x":
                                P.dma("sp", rc2[:, 0:n], ropeC[:, t0:t0 + n], writes=["rc2"])
                                P.dma("sp", rs2[:, 0:n], ropeS[:, t0:t0 + n], writes=["rs2"])
                                P.dma("sp", Q[64:80, :, 0:n], qm_bf[:, t0:t0 + n].unsqueeze(1).to_broadcast([16, NH, n]), writes=[qmk])
                            else:
                                P.op("pool", lambda e: e.memset(Q[64:80, :, :], 0.0), writes=[qmk])
                        units.append(u_load)
                        units += stats_units(x3, n, ["xP"], bufs2, lambda j: hm2[:, j, 0:n], 8, ["hm2"], pjQ, "B7")
                        for g0 in (0, 4):
                            def u_ap(g0=g0):
                                for kc in range(g0, g0 + 4):
                                    tb = bufs2["nt"][kc % 2]
                                    tk = f"ntP{kc % 2}"
                                    P.op("dve", lambda e: e.scalar_tensor_tensor(tb[:, 0:n], x3[:, kc, :], A_[:, kc:kc + 1],
                                                                                 bufs2["rstd"][:, 0:n], ALU.mult, ALU.mult),
                                         reads=["xP", "Prstd", "cs"], writes=[tk])
                                    P.op("dve", lambda e: e.tensor_scalar(hm2[:, kc, 0:n], tb[:, 0:n], B_[:, kc:kc + 1], None, ALU.add),
                                         reads=[tk, "cs"], writes=["hm2"])
                            units.append(u_ap)
                        for c2 in range(4):
                            def u_qa(c2=c2):
                                for kc in range(KC):
                                    P.op("pe", lambda e: e.matmul(pjP[:, 0:n], wqu[:, kc, c2 * 128:(c2 + 1) * 128], hm2[:, kc, 0:n],
                                                                  start=(kc == 0), stop=(kc == KC - 1)),
                                         reads=["wq", "hm2"], writes=["B7"])
                            dst = lambda j, c2=c2: Q[0:64, 2 * c2 + j, 0:n]
                            if kd == "x":
                                def u_qb(c2=c2):
                                    P.op("dve", lambda e: e.tensor_copy(qb2[:, 0:n], pjP[:, 0:n]), reads=[], writes=["qb2", "B7"])
                                    P.op("dve", lambda e: e.tensor_tensor(t12[:, 0:n], pjP[:, 0:n], rc2[:, 0:n], ALU.mult),
                                         reads=["rc2"], writes=["t12", "B7"])

                                def u_qc(c2=c2):
                                    P.op("pe", lambda e: e.matmul(pjQ[:, 0:n], pm_bf, qb2[:, 0:n], start=True, stop=True),
                                         reads=["qb2", "pmi"], writes=["B7"])

                                def u_qd(c2=c2, dst=dst):
                                    P.op("dve", lambda e: e.tensor_tensor(t22[:, 0:n], pjQ[:, 0:n], rs2[:, 0:n], ALU.mult),
                                         reads=["rs2"], writes=["t22", "B7"])
                                    P.op("pool", lambda e: e.tensor_tensor(dst(0), t12[0:64, 0:n], t22[0:64, 0:n], ALU.add),
                                         reads=["t12", "t22"], writes=[qk_])
                                    P.op("pool", lambda e: e.tensor_tensor(dst(1), t12[64:128, 0:n], t22[64:128, 0:n], ALU.add),
                                         reads=["t12", "t22"], writes=[qk_])
                                units += [u_qa, u_qb, u_qc, u_qd]
                            else:
                                def u_qb(c2=c2, dst=dst):
                                    P.op("dve", lambda e: e.tensor_copy(dst(0), pjP[0:64, 0:n]), reads=[], writes=[qk_, "B7"])
                                    P.op("dve", lambda e: e.tensor_copy(dst(1), pjP[64:128, 0:n]), reads=[], writes=[qk_, "B7"])
                                units += [u_qa, u_qb]
                        Lr = 64 if kd == "x" else 256
                        R = n // Lr
                        Lp = Lr + 32
                        ict = ic_x if kd == "x" else ic_c
                        ick = "icx" if kd == "x" else "icc"
                        pA3 = pA[:, 0:R * Lp].rearrange("p (r t) -> p r t", t=Lp)
                        pB3 = pB[:, 0:R * Lp].rearrange("p (r t) -> p r t", t=Lp)
                        pD3 = pD[:, 0:n].rearrange("p (r t) -> p r t", t=Lr)
                        uunits = [(lambda: None) for _ in range(6)]
                        qunits = units
                        units = uunits
                        for g in range(4):
                            def u_pa(g=g):
                                if g == 0:
                                    P.op("dve", lambda e: e.memset(pA[:], 0.0), writes=["pA"])
                                for kc in range(KC):
                                    P.op("pe", lambda e: e.matmul(pjU[:, 0:n], wqu[:, kc, 512 + g * 128:512 + (g + 1) * 128], hm2[:, kc, 0:n],
                                                                  start=(kc == 0), stop=(kc == KC - 1)),
                                         reads=["wu", "hm2"], writes=["B6"])

                            def u_pb(g=g):
                                w_ = [2, 4, 8, 16][g]
                                u3 = pjU[:, 0:n].rearrange("p (r t) -> p r t", t=Lr)
                                P.op("dve", lambda e: e.tensor_copy(pA3[:, :, 16:16 + Lr], u3),
                                     reads=[], writes=["pA", "B6"])
                                cur, curk, oth, othk = pA3, "pA", pB3, "pB"
                                lo = 0
                                sft = 1
                                while sft < w_:
                                    lo += sft
                                    P.op("dve", lambda e: e.tensor_tensor(oth[:, :, lo:Lp], cur[:, :, lo:Lp], cur[:, :, lo - sft:Lp - sft], ALU.add),
                                         reads=[curk], writes=[othk])
                                    cur, curk, oth, othk = oth, othk, cur, curk
                                    sft *= 2
                                st0 = 16 + w_ // 2 - 1
                                icv = ict[:, g * Lr:(g + 1) * Lr].unsqueeze(1).to_broadcast([128, R, Lr])
                                P.op("dve", lambda e: e.tensor_tensor(pD3, cur[:, :, st0:st0 + Lr], icv, ALU.mult),
                                     reads=[curk, ick], writes=["pD"])
                                P.op("dve", lambda e: e.tensor_tensor(dT[:, 0:n], pD[:, 0:n], pjU[:, 0:n], ALU.subtract),
                                     reads=["pD"], writes=["dT", "B6"])
                                if w_ > 2:
                                    P.op("dve", lambda e: e.memset(pA[:], 0.0), reads=[], writes=["pA"])

                            def u_pc(g=g):
                                P.op("pe", lambda e: e.matmul(pjU[:, 0:n], wpl[:, g, :], dT[:, 0:n], start=True, stop=True),
                                     reads=["wpl", "dT"], writes=["B6"])

                            def u_pd(g=g):
                                P.op("dve", lambda e: e.tensor_scalar(MT[:, 4 + g, 0:n], pjU[:, 0:n], psc_sb[:, l * 4 + g:l * 4 + g + 1], None, ALU.mult),
                                     reads=["psc"], writes=[mkp, "B6"])
                            units += [u_pa, u_pb, u_pc, u_pd]
                        return qunits, uunits

                    def tail_units(i):
                        kd, t0, n = qtiles[i]
                        kind = 0 if kd == "x" else 1
                        x3 = xT[:, :, 0:n]
                        MT, mkp, mka = mixT[i % 3], f"mixT{i % 3}p", f"mixT{i % 3}a"
                        C_ = cs(l, kind, 1, 2)
                        units = []

                        def u_load():
                            src = xview(srcX, t0, n) if kd == "x" else xview(srcC, 0, n)
                            P.dma("sp", x3, src, writes=["xT"])
                        units.append(u_load)
                        for oc in range(8):
                            def u_oa(oc=oc):
                                for kc in range(KC):
                                    P.op("pe", lambda e: e.matmul(pjT[:, 0:n], wo[:, kc, oc * 128:(oc + 1) * 128], MT[:, kc, 0:n],
                                                                  start=(kc == 0), stop=(kc == KC - 1)),
                                         reads=["woa", "wob", mkp, mka], writes=["B6"])

                            def u_ob(oc=oc):
                                P.op("dve", lambda e: e.tensor_copy(y_sb[:, oc, 0:n], pjT[:, 0:n]), reads=[], writes=["y", "B6"])
                            units += [u_oa, u_ob]
                        nwo = len(units)
                        y3 = y_sb[:, :, 0:n]
                        units += stats_units(y3, n, ["y"], bufsT, lambda j: MT[:, j, 0:n], 8, [mka, mkp], stT, "B6")
                        for g0 in (0, 4):
                            def u_ap(g0=g0):
                                for kc in range(g0, g0 + 4):
                                    tb = bufsT["nt"][kc % 2]
                                    tk = f"ntT{kc % 2}"
                                    P.op("dve", lambda e: e.scalar_tensor_tensor(tb[:, 0:n], y3[:, kc, :], C_[:, kc:kc + 1],
                                                                                 bufsT["rstd"][:, 0:n], ALU.mult, ALU.mult),
                                         reads=["y", "Trstd", "cs"], writes=[tk])
                                    P.op("pool", lambda e: e.tensor_tensor(x3[:, kc, :], x3[:, kc, :], tb[:, 0:n], ALU.add),
                                         reads=[tk], writes=["xT"])
                            units.append(u_ap)
                        dst = xview(dstX, t0, n) if kd == "x" else xview(dstC, 0, n)
                        units.append(lambda: P.dma("sp", dst, x3, reads=["xT"]))
                        return units, nwo

                    def zip_units(tu, pu):
                        out = []
                        k = 0
                        while k < len(tu) or k < len(pu):
                            if k < len(tu):
                                out.append(tu[k])
                            if k < len(pu):
                                out.append(pu[k])
                            k += 1
                        return out

                    def attn_units(i):
                        kd, t0, n = qtiles[i]
                        Q, qk_, qmk = QTa[i % 2], f"QT{i % 2}", f"QTm{i % 2}"
                        MT, mka = mixT[i % 3], f"mixT{i % 3}a"
                        units = []
                        pending_fin_b = []
                        for bi in range(n // 128):
                            b_ = 0
                            if kd == "x":
                                b_ = t0 // 128 + bi
                                lo_ = 0 if b_ == 19 else 1
                                hi_ = 7 if b_ == 4 else 6
                                lo_c = max(lo_, kb0 + 3 - b_)
                                hi_c = min(hi_, kb1 + 3 - b_)
                            else:
                                lo_c, hi_c = 0, 0
                            nl = hi_c - lo_c
                            assert nl <= 6

                            def qk_exp(h, sb_, pb_, bi=bi, b_=b_, lo_c=lo_c, hi_c=hi_c, nl=nl):
                                q_ap = Q[:, h, bi * 128:(bi + 1) * 128]
                                bk = [Bk[2 * sb_], Bk[2 * sb_ + 1]]
                                kk = [f"B{2 * sb_}", f"B{2 * sb_ + 1}"]
                                if nl > 0:
                                    w0 = min(nl, 4) * 128
                                    P.op("pe", lambda e: e.matmul(bk[0][:, 0:w0], ident8_bf, TTx[:, h, lo_c * 128:lo_c * 128 + w0],
                                                                  start=True, stop=False),
                                         reads=["TTx", "TTxb", "pmi"], writes=[kk[0]])
                                    if nl > 4:
                                        w1 = (nl - 4) * 128
                                        P.op("pe", lambda e: e.matmul(bk[1][:, 0:w1], ident8_bf, TTx[:, h, (lo_c + 4) * 128:(lo_c + 4) * 128 + w1],
                                                                      start=True, stop=False),
                                             reads=["TTx", "TTxb", "pmi"], writes=[kk[1]])
                                for ci in range(lo_c, hi_c):
                                    m = b_ - 3 + ci
                                    s_ = ci - lo_c
                                    last_ = (s_ == min(nl, 4) - 1) or (s_ == nl - 1)
                                    P.op("pe", lambda e: e.matmul(bk[s_ // 4][:, (s_ % 4) * 128:(s_ % 4 + 1) * 128],
                                                                  KTa[:, h, m * 128:(m + 1) * 128], q_ap, start=False, stop=last_),
                                         reads=["KT0", "KT1", "KT2", "KT3", qk_, qmk], writes=[kk[s_ // 4]])
                                for cm in range(2):
                                    P.op("pe", lambda e: e.matmul(bk[1][:, 256 + cm * 128:256 + (cm + 1) * 128],
                                                                  KCa[:, h, cm * 128:(cm + 1) * 128], q_ap, start=True, stop=True),
                                         reads=["KCa", qk_, qmk], writes=[kk[1]])
                                if nl > 0:
                                    w0 = min(nl, 4) * 128
                                    P.op("act", lambda e: e.activation(Pb[pb_][:, 0:w0], bk[0][:, 0:w0], AF.Exp, scale=0.125),
                                         reads=[], writes=[f"Pb{pb_}a", kk[0]])
                                if nl > 4:
                                    P.op("act", lambda e: e.activation(Pb[pb_][:, 512:1024], bk[1][:, 0:512], AF.Exp, scale=0.125),
                                         reads=[], writes=[f"Pb{pb_}b", kk[1]])
                                else:
                                    P.op("act", lambda e: e.activation(Pb[pb_][:, 768:1024], bk[1][:, 256:512], AF.Exp, scale=0.125),
                                         reads=[], writes=[f"Pb{pb_}b", kk[1]])

                            def pv(h, pb_, b_=b_, lo_c=lo_c, hi_c=hi_c):
                                O_ = Oa if h < 4 else Ob
                                okey = "B4" if h < 4 else "B5"
                                ocol = (h % 4) * 65
                                items = [("l", ci) for ci in range(lo_c, hi_c)] + [("c", 0), ("c", 1)]
                                for ii, (tp, ci) in enumerate(items):
                                    if tp == "l":
                                        m = b_ - 3 + ci
                                        s_ = ci - lo_c
                                        lhs = Pb[pb_][:, s_ * 128:(s_ + 1) * 128]
                                        rhs = V_sb[:, m, h, :]
                                    else:
                                        lhs = Pb[pb_][:, 768 + ci * 128:768 + (ci + 1) * 128]
                                        rhs = VC_sb[:, ci, h, :]
                                    P.op("pe", lambda e: e.matmul(O_[:, ocol:ocol + 65], lhs, rhs, start=(ii == 0), stop=(ii == len(items) - 1)),
                                         reads=[f"Pb{pb_}a", f"Pb{pb_}b", "V", "VC"], writes=[okey])

                            def fin_a(bi=bi):
                                Oa3 = Oa[:, 0:260].rearrange("p (h d) -> p h d", d=65)
                                Ob3 = Ob[:, 0:260].rearrange("p (h d) -> p h d", d=65)
                                P.op("act", lambda e: e.activation(rec[:, 0:4], Oa3[:, :, 64], AF.Ln), reads=[], writes=["rec", "B4"])
                                P.op("act", lambda e: e.activation(rec[:, 4:8], Ob3[:, :, 64], AF.Ln), reads=[], writes=["rec", "B5"])
                                P.op("act", lambda e: e.activation(rec[:, 0:8], rec[:, 0:8], AF.Exp, scale=-1.0), reads=["rec"], writes=["rec"])
                                for h in range(NH):
                                    O3 = Oa3 if h < 4 else Ob3
                                    okey = "B4" if h < 4 else "B5"
                                    P.op("act", lambda e: e.activation(mixA[:, h * 64:(h + 1) * 64], O3[:, h % 4, 0:64], AF.Identity,
                                                                       scale=rec[:, h:h + 1]),
                                         reads=["rec"], writes=["mixA", okey])

                            def fin_b(bi=bi):
                                for c4 in range(4):
                                    trp, tkey = (trpA, "B4") if c4 < 2 else (trpB, "B5")
                                    P.op("pe", lambda e: e.transpose(trp[:, (c4 % 2) * 128:(c4 % 2 + 1) * 128], mixA[:, c4 * 128:(c4 + 1) * 128], ident_bf),
                                         reads=["mixA", "pmi"], writes=[tkey])
                                P.op("act", lambda e: e.activation(MT[:, 0:2, bi * 128:(bi + 1) * 128],
                                                                   trpA.rearrange("p (c q) -> p c q", q=128), AF.Identity),
                                     reads=[], writes=[mka, "B4"])
                                P.op("act", lambda e: e.activation(MT[:, 2:4, bi * 128:(bi + 1) * 128],
                                                                   trpB.rearrange("p (c q) -> p c q", q=128), AF.Identity),
                                     reads=[], writes=[mka, "B5"])

                            def step(k, qk_exp=qk_exp, pv=pv):
                                def f_():
                                    if k < NH:
                                        sb_ = scnt[0] % 2
                                        pb_ = scnt[0] % 3
                                        scnt[0] += 1
                                        sbs[k] = pb_
                                        qk_exp(k, sb_, pb_)
                                    if k >= 2:
                                        pv(k - 2, sbs[k - 2])
                                return f_
                            blk_units = [step(k) for k in range(NH + 2)]
                            if pending_fin_b:
                                blk_units[2:2] = [pending_fin_b.pop()]
                            units += blk_units
                            units.append(fin_a)
                            pending_fin_b.append(fin_b)
                        while pending_fin_b:
                            units.append(pending_fin_b.pop())
                        return units

                    def interleave(au, bu):
                        na, nb = len(au), len(bu)
                        ia = ib = 0
                        while ia < na or ib < nb:
                            if ib >= nb or (ia < na and ia * nb <= ib * na):
                                au[ia]()
                                ia += 1
                            else:
                                bu[ib]()
                                ib += 1

                    def zip3(chains):
                        out = []
                        k = 0
                        while any(k < len(c) for c in chains):
                            for c in chains:
                                if k < len(c):
                                    out.append(c[k])
                            k += 1
                        return out

                    q0, u0 = prep_units(0)
                    for u in zip3([q0, u0]):
                        u()
                    carry = []
                    for i in range(NT):
                        tu = tail_units(i - 1)[0] if i > 0 else []
                        qu, uu = prep_units(i + 1) if i + 1 < NT else ([], [])
                        ncar = 3 if (tu and i + 1 < NT) else 0
                        aux = qu[0:1] + carry + zip3([tu[:len(tu) - ncar], qu[1:], uu])
                        carry = tu[len(tu) - ncar:] if ncar else []
                        interleave(attn_units(i), aux)
                    for u in carry + tail_units(NT - 1)[0]:
                        u()
                    P.new_epoch()

        def xt_tiles(src, dst, t0, ntiles, dst_off=0):
            return [dict(src=xview(src, t0 + 512 * i, 512), dst=xview(dst, t0 + 512 * i - dst_off, 512), n=512, kind=0)
                    for i in range(ntiles)]

        def c_tile(src, dst):
            return dict(src=xview(src, 0, CTX), dst=xview(dst, 0, CTX), n=CTX, kind=1)

        def net():
            tl = xt_tiles(XS[0], XS[1], 0, 6)
            yield lambda: ffn_phase(0, 0, [tl[0:2], tl[2:4], tl[4:6] + [c_tile(CS[0], CS[1])]])
            yield lambda: mix_phase(0, XS[1], XS[2], 0, 6, 2, 20, CS[1], CS[2])
            tl2 = xt_tiles(XS[2], XS[3], 256, 5)
            yield lambda: ffn_phase(0, 1, [tl2[0:2], tl2[2:4], tl2[4:5] + [c_tile(CS[2], CS[3])]])
            tl3 = xt_tiles(XS[3], XS[4], 256, 5)
            yield lambda: ffn_phase(1, 0, [tl3[0:2], tl3[2:4], tl3[4:5] + [c_tile(CS[3], CS[4])]])
            yield lambda: mix_phase(1, XS[4], XS[5], 256, 5, 4, 16, CS[4], None)
            tl4 = xt_tiles(XS[5], outT, 512, 4, dst_off=512)
            yield lambda: ffn_phase(1, 1, [tl4[0:2], tl4[2:4]])
        for i, ph_ in enumerate(net()):
            if i >= nphase:
                break
            ph_()
        for e in P.ENG:
            P.drain(e)
        print("ops", P.nops)
    return nc


def _consts(core):
    s = core % 4
    R0 = 32 * s
    lrow = np.arange(WROWS)
    grow = R0 - 8 + lrow
    inv = (np.float32(10000.0) ** (-np.arange(16, dtype=np.float32) / np.float32(16))).astype(np.float32)
    tok_row = np.repeat(grow, GW).astype(np.float32)
    tok_col = np.tile(np.arange(GW), WROWS).astype(np.float32)
    ang_r = (tok_row[:, None] * inv).astype(np.float32)
    ang_c = (tok_col[:, None] * inv).astype(np.float32)
    cos64 = np.concatenate([np.cos(ang_r), np.cos(ang_r), np.cos(ang_c), np.cos(ang_c)], axis=1)
    sin64 = np.concatenate([-np.sin(ang_r), np.sin(ang_r), -np.sin(ang_c), np.sin(ang_c)], axis=1)
    ropeC = np.ascontiguousarray(np.concatenate([cos64, cos64], axis=1).T.astype(np.float32))
    ropeS = np.ascontiguousarray(np.concatenate([sin64, sin64], axis=1).T.astype(np.float32))
    kmask = np.zeros((16, WTOK), np.float32)
    for l_ in range(WROWS):
        kmask[l_ % 16, l_ * GW:(l_ + 1) * GW] = 1.0
    qmask = np.full((16, WTOK), NEG, np.float32)
    for lq in range(WROWS):
        gq = grow[lq]
        if gq < 0 or gq >= ROWS:
            continue
        b = lq // 2
        rs_ = min(max(gq - 4, 0), ROWS - 8)
        for lk in range(2 * b - 6, 2 * b + 8):
            if lk < 0 or lk >= WROWS:
                continue
            gk = R0 - 8 + lk
            if 0 <= gk < ROWS and rs_ <= gk < rs_ + 8:
                qmask[lk % 16, lq * GW:(lq + 1) * GW] = 0.0
    return ropeC, ropeS, kmask, qmask


def _pool_icnt(length):
    t = np.arange(length)
    out = []
    for w in (2, 4, 8, 16):
        lo = np.clip(t - w // 2, 0, length)
        hi = np.clip(t - w // 2 + w, 0, length)
        out.append((np.float32(1.0) / (hi - lo).astype(np.float32)).astype(np.float32))
    return np.tile(np.concatenate(out)[None, :], (128, 1)).astype(np.float32)


def _bias_tables(na_rpb):
    j = np.arange(GW)
    col_start = np.clip(j - 8, 0, GW - 16)
    c = np.arange(GW)
    colvalid = (c[:, None] >= col_start[None, :]) & (c[:, None] < col_start[None, :] + 16)
    dc = np.clip(c[:, None] - j[None, :] + 15, 0, 30)
    bt = np.full((DEPTH, NH, 128, 896), NEG, np.float32)
    for mi in range(7):
        d0 = 2 * (mi - 3)
        for pk in range(2):
            for pq in range(2):
                dr = d0 + pk - pq
                if abs(dr) > 7:
                    continue
                vals = na_rpb[:, :, dr + 7, :][:, :, dc]
                vals = np.where(colvalid[None, None], vals, np.float32(NEG))
                bt[:, :, pk * 64:(pk + 1) * 64, mi * 128 + pq * 64: mi * 128 + (pq + 1) * 64] = vals
    return bt


_NC_CACHE = {}


def prepare_inputs(x, c, ctx, c_ctx, w_mod, b_mod, norm_g, w_ffn_gate_up, w_ffn_down, w_in, w_out, na_rpb,
                   w_pool, pool_scale):
    f = np.float32
    x = np.asarray(x, f); c = np.asarray(c, f); ctx = np.asarray(ctx, f); c_ctx = np.asarray(c_ctx, f)
    w_mod = np.ascontiguousarray(np.asarray(w_mod, f)); b_mod = np.asarray(b_mod, f); norm_g = np.asarray(norm_g, f)
    wgu = np.ascontiguousarray(np.asarray(w_ffn_gate_up, f)); wd = np.ascontiguousarray(np.asarray(w_ffn_down, f))
    w_in = np.ascontiguousarray(np.asarray(w_in, f)); w_out = np.ascontiguousarray(np.asarray(w_out, f))
    na_rpb = np.asarray(na_rpb, f); w_pool = np.ascontiguousarray(np.asarray(w_pool, f)); pool_scale = np.asarray(pool_scale, f)
    bm = np.ascontiguousarray(b_mod.reshape(DEPTH, 72, 128).transpose(2, 0, 1).reshape(128, DEPTH * 72))
    ng = np.ascontiguousarray(norm_g.reshape(DEPTH, 6 * 8, 128).transpose(2, 0, 1).reshape(128, DEPTH * 48))
    psc = np.ascontiguousarray(pool_scale.reshape(DEPTH, 4, 128).transpose(2, 0, 1).reshape(128, DEPTH * 4))
    bt = _bias_tables(na_rpb)
    icx = _pool_icnt(64)
    icc = _pool_icnt(256)
    pm = np.zeros((128, 128), f)
    for m in range(128):
        k = m + 16 if (m % 32) < 16 else m - 16
        pm[k, m] = 1.0
    pmi = np.ascontiguousarray(np.concatenate([pm, np.eye(128, dtype=f), 8.0 * np.eye(128, dtype=f)], axis=1))
    in_maps = []
    for core in range(8):
        b = core // 4
        R0 = 32 * (core % 4)
        xw = np.zeros((WROWS, GW, D), f)
        g0, g1 = R0 - 8, R0 - 8 + WROWS
        a0, a1 = max(g0, 0), min(g1, ROWS)
        xw[a0 - g0:a1 - g0] = x[b].reshape(ROWS, GW, D)[a0:a1]
        xT = np.ascontiguousarray(xw.reshape(WTOK, D).T)
        cT = np.ascontiguousarray(ctx[b].T)
        sT = np.ascontiguousarray(np.stack([c[b].reshape(8, 128).T, c_ctx.reshape(8, 128).T], axis=2).reshape(128, 16))
        ropeC, ropeS, kmask, qmask = _consts(core)
        in_maps.append(dict(xT=xT, cT=cT, sT=sT, wmod=w_mod, bm=bm, ng=ng, wgu=wgu, wd=wd, win=w_in, wout=w_out,
                            wpool=w_pool, psc=psc, bt=bt, ropeC=ropeC, ropeS=ropeS, kmask=kmask, qmask=qmask,
                            icx=icx, icc=icc, pmi=pmi))
    return in_maps


def kernel(**inputs):
    in_maps = prepare_inputs(**inputs)
    if "nc" not in _NC_CACHE:
        _NC_CACHE["nc"] = build()
    nc = _NC_CACHE["nc"]
    res = run_bass_kernel_spmd(nc, in_maps, core_ids=list(range(8)))
    out = np.zeros((2, 8192, D), np.float32)
    for core in range(8):
        b = core // 4
        R0 = 32 * (core % 4)
        o = np.asarray(res.results[core]["outT"], np.float32)
        out[b, R0 * GW:(R0 + 32) * GW, :] = o.T
    return out
```

```python
import contextlib
import numpy as np
import concourse.bass as bass
import concourse.mybir as mybir
from concourse.bass_utils import run_bass_kernel_spmd

F32 = mybir.dt.float32
BF16 = mybir.dt.bfloat16
ALU = mybir.AluOpType
AF = mybir.ActivationFunctionType

D = 1024
KC = 8
DEPTH = 2
GW = 64
ROWS = 128
WROWS = 48
WTOK = WROWS * GW
NBLK = WROWS // 2
CTX = 256
DFF = 2816
HC = 22
NH = 8
NEG = -30000.0
EPS = 1e-6


class Prog:
    ENG = ("pe", "act", "dve", "pool", "sp")

    def __init__(self, nc, stack, n_dma_sems=24):
        self.nc = nc
        self.stack = stack
        self.eng = {"pe": nc.tensor, "act": nc.scalar, "dve": nc.vector,
                    "pool": nc.gpsimd, "sp": nc.sync}
        self.sem = {}
        self.cnt = {}
        self.epoch = 0
        for e in self.ENG:
            self.sem[e] = stack.enter_context(nc.semaphore(f"s_{e}_0"))
            self.cnt[e] = 0
        self.dsem = [stack.enter_context(nc.semaphore(f"d_{i}")) for i in range(n_dma_sems)]
        self.dcnt = [0] * n_dma_sems
        self.n_sw = n_dma_sems // 3
        self.dnext_sw = 0
        self.dnext_hw = 0
        self.waited = {e: {} for e in self.ENG}
        self.buf = {}
        self.nops = 0

    def _wait(self, e, tok):
        if tok is None:
            return
        if tok[0] == 'e':
            _, src, ep, seq = tok
            if ep != self.epoch:
                return
            if src == e and e == "pe":
                return
            key = ('e', src)
            if self.waited[e].get(key, 0) >= seq:
                return
            self.eng[e].wait_ge(self.sem[src], seq)
            self.waited[e][key] = seq
        else:
            _, idx, val = tok
            key = ('d', idx)
            if self.waited[e].get(key, 0) >= val:
                return
            self.eng[e].wait_ge(self.dsem[idx], val)
            self.waited[e][key] = val

    def _deps(self, e, reads, writes):
        for k in reads:
            st = self.buf.get(k)
            if st:
                self._wait(e, st[0])
        for k in writes:
            st = self.buf.get(k)
            if st:
                self._wait(e, st[0])
                for t in st[1]:
                    self._wait(e, t)

    def _update(self, tok, reads, writes):
        for k in reads:
            st = self.buf.setdefault(k, [None, []])
            st[1].append(tok)
            if len(st[1]) > 48:
                last = {}
                keep = []
                for t in st[1]:
                    if t[0] == 'e':
                        last[t[1]] = t
                    else:
                        keep.append(t)
                st[1] = keep[-40:] + list(last.values())
        for k in writes:
            self.buf[k] = [tok, []]

    def op(self, e, fn, reads=(), writes=()):
        self._deps(e, reads, writes)
        ins = fn(self.eng[e])
        self.cnt[e] += 1
        ins.then_inc(self.sem[e], 1)
        tok = ('e', e, self.epoch, self.cnt[e])
        self._update(tok, reads, writes)
        self.nops += 1
        return tok

    def dma(self, q, out, in_, reads=(), writes=(), **kw):
        self._deps(q, reads, writes)
        if q == "pool":
            idx = self.dnext_sw
            self.dnext_sw = (self.dnext_sw + 1) % self.n_sw
        else:
            idx = self.n_sw + self.dnext_hw
            self.dnext_hw = (self.dnext_hw + 1) % (len(self.dsem) - self.n_sw)
        if self.dcnt[idx] > 0:
            self._wait(q, ('d', idx, self.dcnt[idx]))
        ins = self.eng[q].dma_start(out=out, in_=in_, **kw)
        self.dcnt[idx] += 16
        ins.then_inc(self.dsem[idx], 16)
        tok = ('d', idx, self.dcnt[idx])
        self._update(tok, reads, writes)
        self.nops += 1
        return tok

    def drain(self, e):
        for idx in range(len(self.dsem)):
            if self.dcnt[idx] > 0:
                self._wait(e, ('d', idx, self.dcnt[idx]))
        for src in self.ENG:
            if self.cnt[src] > 0 and src != e:
                self._wait(e, ('e', src, self.epoch, self.cnt[src]))

    def new_epoch(self):
        for e in self.ENG:
            self.drain(e)
        self.epoch += 1
        for e in self.ENG:
            self.sem[e] = self.stack.enter_context(self.nc.semaphore(f"s_{e}_{self.epoch}"))
            self.cnt[e] = 0
        for e in self.ENG:
            self.waited[e] = {k: v for k, v in self.waited[e].items() if k[0] == 'd'}
        self.buf = {}


def build(debug=False, nphase=99):
    nc = bass.Bass("TRN2", target_bir_lowering=False)

    def din(name, shape):
        return nc.dram_tensor(name, list(shape), F32, kind="ExternalInput").ap()

    xT = din("xT", [D, WTOK])
    cT = din("cT", [D, CTX])
    sT = din("sT", [128, 16])
    wmod = din("wmod", [DEPTH, D, 9 * D])
    bm = din("bm", [128, DEPTH * 72])
    ng = din("ng", [128, DEPTH * 48])
    wgu = din("wgu", [DEPTH, 2, D, 2 * DFF])
    wd = din("wd", [DEPTH, 2, DFF, D])
    win = din("win", [DEPTH, D, 2048])
    wout = din("wout", [DEPTH, D, D])
    wpool = din("wpool", [DEPTH, 4, 128, 128])
    psc = din("psc", [128, DEPTH * 4])
    bt = din("bt", [DEPTH, NH, 128, 896])
    ropeC = din("ropeC", [128, WTOK])
    ropeS = din("ropeS", [128, WTOK])
    kmask = din("kmask", [16, WTOK])
    qmask = din("qmask", [16, WTOK])
    icx = din("icx", [128, 4 * 64])
    icc = din("icc", [128, 4 * 256])
    pmi = din("pmi", [128, 384])

    skind = "ExternalOutput" if debug else "Internal"
    XS = [xT] + [nc.dram_tensor(f"X{i}", [D, WTOK], F32, kind=skind).ap() for i in range(1, 6)]
    CS = [cT] + [nc.dram_tensor(f"C{i}", [D, CTX], F32, kind=skind).ap() for i in range(1, 5)]
    outT = nc.dram_tensor("outT", [D, 2048], F32, kind="ExternalOutput").ap()
    qm_bf = nc.dram_tensor("qm_bf", [16, WTOK], BF16, kind="Internal").ap()

    def xview(ap, t0, n):
        return ap.rearrange("(kc p) t -> p kc t", p=128)[:, :, t0:t0 + n]

    with contextlib.ExitStack() as st:
        P = Prog(nc, st)

        uid = [0]

        def SB(stack, name, shape, dt):
            uid[0] += 1
            return stack.enter_context(nc.sbuf_tensor(f"{name}_{uid[0]}", list(shape), dt))

        def PS(stack, name, shape, dt=F32):
            uid[0] += 1
            return stack.enter_context(nc.psum_tensor(f"{name}_{uid[0]}", list(shape), dt))

        ones_bf = SB(st, "ones_bf", [128, 128], BF16)
        pmi_bf = SB(st, "pmi_bf", [128, 384], BF16)
        ng_sb = SB(st, "ng_sb", [128, DEPTH * 48], F32)
        psc_sb = SB(st, "psc_sb", [128, DEPTH * 4], F32)
        eps_sb = SB(st, "eps_sb", [128, 1], F32)
        cs_sb = SB(st, "cs_sb", [128, DEPTH * 2 * 3 * 3 * 8], F32)

        def cs(l, kind, sub, which):
            o = (((l * 2 + kind) * 3 + sub) * 3 + which) * 8
            return cs_sb[:, o:o + 8]

        pm_bf = pmi_bf[:, 0:128]
        ident_bf = pmi_bf[:, 128:256]
        ident8_bf = pmi_bf[:, 256:384]

        P.op("dve", lambda e: e.memset(ones_bf[:], 1.0), writes=["ones"])
        P.op("dve", lambda e: e.memset(eps_sb[:], EPS), writes=["eps"])
        P.dma("pool", pmi_bf[:], pmi, writes=["pmi"])
        P.dma("sp", ng_sb[:], ng, writes=["ng"])
        P.dma("sp", psc_sb[:], psc, writes=["psc"])
        P.dma("pool", qm_bf, qmask)

        s_bf = SB(st, "s_bf", [128, 16], BF16)
        bm_sb = SB(st, "bm_sb", [128, DEPTH * 72], F32)
        s3 = s_bf[:].rearrange("p (k t) -> p k t", t=2)

        def mod_units(l, wbufs, wkeys, mps, mkey, mod_sb, tmp8, gw):
            units = []
            ng_ = 1024 // gw
            wsrc = wmod[l].rearrange("(kc p) n -> p kc n", p=128)
            cnt = [0]
            for gi in range(9 * ng_):
                def u(gi=gi):
                    i = cnt[0] % len(wbufs)
                    cnt[0] += 1
                    w, wk = wbufs[i], wkeys[i]
                    P.dma("pool", w[:, :, 0:gw], wsrc[:, :, gi * gw:(gi + 1) * gw], writes=[wk])
                    for oc in range(gw // 128):
                        col = (gi * (gw // 128) + oc) * 2
                        for kc in range(KC):
                            P.op("pe", lambda e: e.matmul(mps[:, col:col + 2], w[:, kc, oc * 128:(oc + 1) * 128],
                                                          s3[:, kc, :], start=(kc == 0), stop=(kc == KC - 1)),
                                 reads=[wk, "s_bf"], writes=[mkey])
                units.append(u)

            def fin_():
                m3 = mps[:, 0:144].rearrange("p (m t) -> p m t", t=2)
                for kind in range(2):
                    o = kind * 72
                    P.op("dve", lambda e: e.tensor_tensor(mod_sb[:, o:o + 72], m3[:, :, kind],
                                                          bm_sb[:, l * 72:(l + 1) * 72], ALU.add),
                         reads=["bm"], writes=[f"mod{l}{kind}", mkey])
                    for sub in range(3):
                        pre, post = [(0, 1), (2, 3), (4, 5)][sub]
                        coef = [0.5, 1.0, 0.5][sub]
                        sh = mod_sb[:, o + (3 * sub) * 8: o + (3 * sub) * 8 + 8]
                        sc_ = mod_sb[:, o + (3 * sub + 1) * 8: o + (3 * sub + 1) * 8 + 8]
                        gt = mod_sb[:, o + (3 * sub + 2) * 8: o + (3 * sub + 2) * 8 + 8]
                        gpre = ng_sb[:, l * 48 + pre * 8: l * 48 + pre * 8 + 8]
                        gpost = ng_sb[:, l * 48 + post * 8: l * 48 + post * 8 + 8]
                        P.op("dve", lambda e: e.tensor_scalar(tmp8[:], sc_, 1.0, None, ALU.add),
                             reads=[f"mod{l}{kind}"], writes=["tmp8"])
                        P.op("dve", lambda e: e.tensor_tensor(cs(l, kind, sub, 0), tmp8[:], gpre, ALU.mult),
                             reads=["tmp8", "ng"], writes=["cs"])
                        P.op("dve", lambda e: e.tensor_copy(cs(l, kind, sub, 1), sh),
                             reads=[f"mod{l}{kind}"], writes=["cs"])
                        P.op("dve", lambda e: e.scalar_tensor_tensor(cs(l, kind, sub, 2), gt, coef, gpost,
                                                                     ALU.mult, ALU.mult),
                             reads=[f"mod{l}{kind}", "ng"], writes=["cs"])
            units.append(fin_)
            return units

        with contextlib.ExitStack() as ph:
            s_sb = SB(ph, "s_sb", [128, 16], F32)
            mod_sb0 = SB(ph, "mod_sb", [128, 2 * 72], F32)
            tmp80 = SB(ph, "tmp8", [128, 8], F32)
            wm = [SB(ph, f"wm{i}", [128, KC, 1024], BF16) for i in range(2)]
            mps0 = PS(ph, "mps", [128, 512])
            P.dma("sp", s_sb[:], sT, writes=["s_sb"])
            P.dma("sp", bm_sb[:], bm, writes=["bm"])
            P.op("act", lambda e: e.activation(s_bf[:], s_sb[:], AF.Silu), reads=["s_sb"], writes=["s_bf"])
            for u in mod_units(0, wm, ["wm0", "wm1"], mps0, "mps", mod_sb0, tmp80, 1024):
                u()
            P.new_epoch()

        def rms_stats(src3, n, srckeys, bufs):
            for kc in range(KC):
                sq = bufs["sq"][kc % 2]
                sk = f"sq{kc % 2}"
                P.op("act", lambda e: e.activation(sq[:, 0:n], src3[:, kc, :], AF.Square),
                     reads=srckeys, writes=[sk])
                P.op("pe", lambda e: e.matmul(bufs["stat_ps"][:, 0:n], ones_bf[:], sq[:, 0:n],
                                              start=(kc == 0), stop=(kc == KC - 1)),
                     reads=[sk, "ones"], writes=[bufs["stat_key"]])
            P.op("act", lambda e: e.activation(bufs["rt"][:, 0:n], bufs["stat_ps"][:, 0:n], AF.Sqrt,
                                               bias=eps_sb[:, 0:1], scale=1.0 / D),
                 reads=["eps"], writes=["rt", bufs["stat_key"]])
            P.op("dve", lambda e: e.reciprocal(bufs["rstd"][:, 0:n], bufs["rt"][:, 0:n]),
                 reads=["rt"], writes=["rstd"])

        def rms_stats_b(src3, n, srckeys, bufs, sq_fn, nslots, sqkeys):
            for g0 in range(0, KC, nslots):
                for j in range(nslots):
                    P.op("act", lambda e: e.activation(sq_fn(j), src3[:, g0 + j, :], AF.Square),
                         reads=srckeys, writes=sqkeys)
                for j in range(nslots):
                    kc = g0 + j
                    P.op("pe", lambda e: e.matmul(bufs["stat_ps"][:, 0:n], ones_bf[:], sq_fn(j),
                                                  start=(kc == 0), stop=(kc == KC - 1)),
                         reads=list(sqkeys) + ["ones"], writes=[bufs["stat_key"]])
            P.op("act", lambda e: e.activation(bufs["rt"][:, 0:n], bufs["stat_ps"][:, 0:n], AF.Sqrt,
                                               bias=eps_sb[:, 0:1], scale=1.0 / D),
                 reads=["eps"], writes=["rt", bufs["stat_key"]])
            P.op("dve", lambda e: e.reciprocal(bufs["rstd"][:, 0:n], bufs["rt"][:, 0:n]),
                 reads=["rt"], writes=["rstd"])

        def norm_mod(src3, n, srckeys, bufs, A, B, dst3, dstkeys, batched=False):
            if batched:
                rms_stats_b(src3, n, srckeys, bufs, lambda j: dst3[:, j, :], KC, dstkeys)
            else:
                rms_stats(src3, n, srckeys, bufs)
            for kc in range(KC):
                tb = bufs["nt"][kc % 2]
                tk = f"nt{kc % 2}"
                P.op("dve", lambda e: e.scalar_tensor_tensor(tb[:, 0:n], src3[:, kc, :], A[:, kc:kc + 1],
                                                             bufs["rstd"][:, 0:n], ALU.mult, ALU.mult),
                     reads=list(srckeys) + ["rstd", "cs"], writes=[tk])
                P.op("act", lambda e: e.activation(dst3[:, kc, :], tb[:, 0:n], AF.Identity,
                                                   bias=B[:, kc:kc + 1], scale=1.0),
                     reads=[tk, "cs"], writes=dstkeys)

        def post_res(y3, x3, n, ykeys, xkeys, bufs, C, dst_dram3, q="sp", sqp=None):
            if sqp is not None:
                rms_stats_b(y3, n, ykeys, bufs, lambda j: sqp[:, j, 0:n], 4, ["sqp"])
            else:
                rms_stats(y3, n, ykeys, bufs)
            for kc in range(KC):
                tb = bufs["nt"][kc % 2]
                tk = f"nt{kc % 2}"
                P.op("dve", lambda e: e.scalar_tensor_tensor(tb[:, 0:n], y3[:, kc, :], C[:, kc:kc + 1],
                                                             bufs["rstd"][:, 0:n], ALU.mult, ALU.mult),
                     reads=list(ykeys) + ["rstd", "cs"], writes=[tk])
                P.op("pool", lambda e: e.tensor_tensor(x3[:, kc, :], x3[:, kc, :], tb[:, 0:n], ALU.add),
                     reads=[tk] + list(xkeys), writes=xkeys)
            P.dma(q, dst_dram3, x3, reads=xkeys)

        def ffn_phase(l, f, passes):
            sub = 0 if f == 0 else 2
            with contextlib.ExitStack() as ph:
                TP = 1280
                h_sb = SB(ph, "h_sb", [128, KC, TP], BF16)
                act_sb = SB(ph, "act_sb", [128, HC, TP], BF16)
                y_sb = SB(ph, "y_sb", [128, KC, TP], F32)
                xt = [SB(ph, f"xt{i}", [128, KC, 512], F32) for i in range(2)]
                NG = 2
                gu = [SB(ph, f"gu{i}", [128, 2, KC, 256], BF16) for i in range(NG)]
                dw = [SB(ph, f"dw{i}", [128, HC, 256], BF16) for i in range(2)]
                sg = [SB(ph, f"sg{i}", [128, 512], F32) for i in range(2)]
                sqp = SB(ph, "sqp", [128, 4, 512], BF16)
                bufs = {
                    "nt": [SB(ph, f"nt{i}", [128, 512], F32) for i in range(2)],
                    "rt": SB(ph, "rt", [128, 512], F32),
                    "rstd": SB(ph, "rstd", [128, 512], F32),
                    "stat_ps": PS(ph, "stat_ps", [128, 512]),
                    "stat_key": "stat_ps",
                }
                gps = [PS(ph, f"gps{i}", [128, 512]) for i in range(2)]
                ups = [PS(ph, f"ups{i}", [128, 512]) for i in range(2)]
                dps = [PS(ph, f"dps{i}", [128, 512]) for i in range(2)]
                wgu3 = wgu[l, f].rearrange("(kc p) n -> p kc n", p=128)
                wd3 = wd[l, f].rearrange("(kc p) n -> p kc n", p=128)
                st_ = dict(g=0, d=0, pc=0, xc=0)

                def offs_of(tiles):
                    offs, o = [], 0
                    for t in tiles:
                        offs.append(o)
                        o += t["n"]
                    assert o <= TP
                    return offs

                def load_gu(j, slot):
                    P.dma("pool", gu[slot][:, 0], wgu3[:, :, j * 256:(j + 1) * 256], writes=[f"gu{slot}g"])
                    P.dma("pool", gu[slot][:, 1], wgu3[:, :, DFF + j * 256:DFF + (j + 1) * 256], writes=[f"gu{slot}u"])

                def load_dw(op_, slot):
                    P.dma("pool", dw[slot][:, 0:11, :], wd3[:, 0:11, op_ * 256:(op_ + 1) * 256], writes=[f"dw{slot}a"])
                    P.dma("pool", dw[slot][:, 11:22, :], wd3[:, 11:22, op_ * 256:(op_ + 1) * 256], writes=[f"dw{slot}b"])

                def stat_tail(n):
                    P.op("act", lambda e: e.activation(bufs["rt"][:, 0:n], bufs["stat_ps"][:, 0:n], AF.Sqrt,
                                                       bias=eps_sb[:, 0:1], scale=1.0 / D),
                         reads=["eps"], writes=["rt", "stat_ps"])
                    P.op("dve", lambda e: e.reciprocal(bufs["rstd"][:, 0:n], bufs["rt"][:, 0:n]),
                         reads=["rt"], writes=["rstd"])

                def norm_units(tiles):
                    offs = offs_of(tiles)
                    units = []
                    for ti, t in enumerate(tiles):
                        n = t["n"]
                        xi = st_["xc"] % 2
                        st_["xc"] += 1
                        x3 = xt[xi][:, :, 0:n]
                        xk = f"xt{xi}"
                        h3 = h_sb[:, :, offs[ti]:offs[ti] + n]
                        hk = f"h{ti}"
                        A_, B_ = cs(l, t["kind"], sub, 0), cs(l, t["kind"], sub, 1)

                        def u0(x3=x3, xk=xk, t=t, n=n, h3=h3, hk=hk):
                            P.dma("sp", x3, t["src"], writes=[xk])
                            for kc in range(KC):
                                P.op("act", lambda e: e.activation(h3[:, kc, :], x3[:, kc, :], AF.Square), reads=[xk], writes=[hk])

                        def u1(n=n, h3=h3, hk=hk):
                            for kc in range(KC):
                                P.op("pe", lambda e: e.matmul(bufs["stat_ps"][:, 0:n], ones_bf[:], h3[:, kc, :],
                                                              start=(kc == 0), stop=(kc == KC - 1)),
                                     reads=[hk, "ones"], writes=["stat_ps"])

                        def u2(x3=x3, xk=xk, n=n, h3=h3, hk=hk, A_=A_, B_=B_):
                            stat_tail(n)
                            for kc in range(KC):
                                tb = bufs["nt"][kc % 2]
                                tk = f"nt{kc % 2}"
                                P.op("dve", lambda e: e.scalar_tensor_tensor(tb[:, 0:n], x3[:, kc, :], A_[:, kc:kc + 1],
                                                                             bufs["rstd"][:, 0:n], ALU.mult, ALU.mult),
                                     reads=[xk, "rstd", "cs"], writes=[tk])
                                P.op("act", lambda e: e.activation(h3[:, kc, :], tb[:, 0:n], AF.Identity,
                                                                   bias=B_[:, kc:kc + 1], scale=1.0),
                                     reads=[tk, "cs"], writes=[hk])
                        units += [u0, u1, u2]
                    return units

                def post_units(tiles):
                    offs = offs_of(tiles)
                    units = []
                    for ti, t in enumerate(tiles):
                        n = t["n"]
                        xi = st_["xc"] % 2
                        st_["xc"] += 1
                        x3 = xt[xi][:, :, 0:n]
                        xk = f"xt{xi}"
                        y3 = y_sb[:, :, offs[ti]:offs[ti] + n]
                        yk = f"y{ti}"
                        C_ = cs(l, t["kind"], sub, 2)

                        def sq(g0, y3=y3, yk=yk, n=n):
                            for j in range(4):
                                P.op("act", lambda e: e.activation(sqp[:, j, 0:n], y3[:, g0 + j, :], AF.Square), reads=[yk], writes=["sqp"])

                        def mm(g0, n=n):
                            for j in range(4):
                                kc = g0 + j
                                P.op("pe", lambda e: e.matmul(bufs["stat_ps"][:, 0:n], ones_bf[:], sqp[:, j, 0:n],
                                                              start=(kc == 0), stop=(kc == KC - 1)),
                                     reads=["sqp", "ones"], writes=["stat_ps"])

                        def u0(x3=x3, xk=xk, t=t, sq=sq):
                            P.dma("sp", x3, t["src"], writes=[xk])
                            sq(0)

                        def u1(sq=sq, mm=mm):
                            mm(0)
                            sq(4)

                        def u2(mm=mm, n=n):
                            mm(4)
                            stat_tail(n)

                        def u3(x3=x3, xk=xk, y3=y3, yk=yk, n=n, C_=C_, t=t):
                            for kc in range(KC):
                                tb = bufs["nt"][kc % 2]
                                tk = f"nt{kc % 2}"
                                P.op("dve", lambda e: e.scalar_tensor_tensor(tb[:, 0:n], y3[:, kc, :], C_[:, kc:kc + 1],
                                                                             bufs["rstd"][:, 0:n], ALU.mult, ALU.mult),
                                     reads=[yk, "rstd", "cs"], writes=[tk])
                                P.op("dve", lambda e: e.tensor_tensor(x3[:, kc, :], x3[:, kc, :], tb[:, 0:n], ALU.add),
                                     reads=[tk, xk], writes=[xk])
                            P.dma("sp", t["dst"], x3, reads=[xk])
                        units += [u0, u1, u2, u3]
                    return units

                def emit_norm(tiles):
                    for u in norm_units(tiles):
                        u()

                def emit_post(tiles):
                    for u in post_units(tiles):
                        u()

                def emit_gu(tiles, inject, has_next):
                    offs = offs_of(tiles)
                    g0 = st_["g"]
                    for j in range(11):
                        slot = (g0 + j) % NG
                        if j + 1 < 11:
                            load_gu(j + 1, (g0 + j + 1) % NG)
                        elif has_next:
                            load_gu(0, (g0 + 11) % NG)
                        if j == 8:
                            load_dw(0, st_["d"] % 2)
                        if j == 10:
                            load_dw(1, (st_["d"] + 1) % 2)
                        for ti, t in enumerate(tiles):
                            n = t["n"]
                            hs = h_sb[:, :, offs[ti]:offs[ti] + n]
                            for s_ in range(2):
                                b = st_["pc"] % 2
                                st_["pc"] += 1
                                for kc in range(KC):
                                    P.op("pe", lambda e: e.matmul(gps[b][:, 0:n], gu[slot][:, 0, kc, s_ * 128:(s_ + 1) * 128],
                                                                  hs[:, kc, :], start=(kc == 0), stop=(kc == KC - 1)),
                                         reads=[f"gu{slot}g", f"h{ti}"], writes=[f"gps{b}"])
                                for kc in range(KC):
                                    P.op("pe", lambda e: e.matmul(ups[b][:, 0:n], gu[slot][:, 1, kc, s_ * 128:(s_ + 1) * 128],
                                                                  hs[:, kc, :], start=(kc == 0), stop=(kc == KC - 1)),
                                         reads=[f"gu{slot}u", f"h{ti}"], writes=[f"ups{b}"])
                                P.op("act", lambda e: e.activation(sg[b][:, 0:n], gps[b][:, 0:n], AF.Silu),
                                     reads=[], writes=[f"sg{b}", f"gps{b}"])
                                hi = j * 2 + s_
                                P.op("dve", lambda e: e.tensor_tensor(act_sb[:, hi, offs[ti]:offs[ti] + n],
                                                                      sg[b][:, 0:n], ups[b][:, 0:n], ALU.mult),
                                     reads=[f"sg{b}"], writes=[f"act{ti}", f"ups{b}"])
                            if j >= 1 and inject:
                                inject.pop(0)()
                    while inject:
                        inject.pop(0)()
                    st_["g"] += 11

                def emit_down(tiles, inject):
                    offs = offs_of(tiles)
                    d0 = st_["d"]
                    for op_ in range(4):
                        slot = (d0 + op_) % 2
                        for ti, t in enumerate(tiles):
                            n = t["n"]
                            for s_ in range(2):
                                b = st_["pc"] % 2
                                st_["pc"] += 1
                                for hc in range(HC):
                                    P.op("pe", lambda e: e.matmul(dps[b][:, 0:n], dw[slot][:, hc, s_ * 128:(s_ + 1) * 128],
                                                                  act_sb[:, hc, offs[ti]:offs[ti] + n],
                                                                  start=(hc == 0), stop=(hc == HC - 1)),
                                         reads=[f"dw{slot}a", f"dw{slot}b", f"act{ti}"], writes=[f"dps{b}"])
                                oc = op_ * 2 + s_
                                if (oc % 2) == 0:
                                    P.op("act", lambda e: e.activation(y_sb[:, oc, offs[ti]:offs[ti] + n], dps[b][:, 0:n],
                                                                       AF.Identity),
                                         reads=[], writes=[f"y{ti}", f"dps{b}"])
                                else:
                                    P.op("dve", lambda e: e.tensor_copy(y_sb[:, oc, offs[ti]:offs[ti] + n], dps[b][:, 0:n]),
                                         reads=[], writes=[f"y{ti}", f"dps{b}"])
                            if inject:
                                inject.pop(0)()
                        if op_ + 2 < 4:
                            load_dw(op_ + 2, (d0 + op_ + 2) % 2)
                    while inject:
                        inject.pop(0)()
                    st_["d"] += 4

                load_gu(0, st_["g"] % NG)
                emit_norm(passes[0])
                for p, tiles in enumerate(passes):
                    inj = post_units(passes[p - 1]) if p > 0 else []
                    emit_gu(tiles, inj, p + 1 < len(passes))
                    ninj = norm_units(passes[p + 1]) if p + 1 < len(passes) else []
                    emit_down(tiles, ninj)
                emit_post(passes[-1])
                P.new_epoch()

        def mix_phase(l, srcX, dstX, kv_t0, kv_ntiles, q_blk0, q_nblk, srcC, dstC):
            with contextlib.ExitStack() as ph:
                KTa = SB(ph, "KTa", [80, NH, WTOK], BF16)
                KCa = SB(ph, "KCa", [80, NH, CTX], BF16)
                V_sb = SB(ph, "V_sb", [128, NBLK, NH, 65], BF16)
                VC_sb = SB(ph, "VC_sb", [128, 2, NH, 65], BF16)
                TTx = SB(ph, "TTx", [128, NH, 896], BF16)
                wpl = SB(ph, "wpl", [128, 4, 128], BF16)
                Bk = [PS(ph, f"B{i}", [128, 512]) for i in range(8)]
                m1s = contextlib.ExitStack()

                P.op("dve", lambda e: e.memset(V_sb[:, :, :, 64:65], 1.0), writes=["V"])
                P.op("dve", lambda e: e.memset(VC_sb[:, :, :, 64:65], 1.0), writes=["VC"])
                P.op("pool", lambda e: e.memset(KCa[64:80, :, :], 0.0), writes=["KCa"])
                for h in range(NH):
                    P.dma("pool", KTa[64:80, h, :], kmask, writes=[f"KTm{h}"])
                P.dma("pool", wpl[:], wpool[l].rearrange("g c d -> c g d"), writes=["wpl"])
                btl = bt[l].rearrange("h p c -> p h c")
                P.dma("pool", TTx[:, 0:4, :], btl[:, 0:4, :], writes=["TTx"])
                P.dma("pool", TTx[:, 4:8, :], btl[:, 4:8, :], writes=["TTxb"])

                win3 = win[l].rearrange("(kc p) n -> p kc n", p=128)

                with contextlib.ExitStack() as p1:
                    wkv = SB(p1, "wkv", [128, KC, 1024], BF16)
                    P.dma("pool", wkv[:, 0:4, :], win3[:, 0:4, 512:1536], writes=["wkva"])
                    P.dma("pool", wkv[:, 4:8, :], win3[:, 4:8, 512:1536], writes=["wkvb"])
                    R_ = []
                    for p_ in range(2):
                        R_.append(dict(
                            xt=SB(p1, f"xt{p_}", [128, KC, 512], F32), hm=SB(p1, f"hm{p_}", [128, KC, 512], BF16),
                            rc=SB(p1, f"rc{p_}", [128, 512], F32), rs=SB(p1, f"rs{p_}", [128, 512], F32),
                            qb=SB(p1, f"qb{p_}", [128, 512], BF16), t1=SB(p1, f"t1{p_}", [128, 512], F32),
                            t2=SB(p1, f"t2{p_}", [128, 512], F32),
                            nt=[SB(p1, f"nt{p_}{i}", [128, 512], F32) for i in range(2)],
                            rt=SB(p1, f"rt{p_}", [128, 512], F32), rstd=SB(p1, f"rstd{p_}", [128, 512], F32),
                            K=Bk[3 * p_], R=Bk[3 * p_ + 1], V=Bk[3 * p_ + 2],
                            Kk=f"B{3 * p_}", Rk=f"B{3 * p_ + 1}", Vk=f"B{3 * p_ + 2}", p=str(p_)))
                    kvtiles = [("x", kv_t0 + 512 * i, 512) for i in range(kv_ntiles)] + [("c", 0, CTX)]

                    def m1_units(ti, r):
                        kd, t0, n = kvtiles[ti]
                        kind = 0 if kd == "x" else 1
                        pz = r["p"]
                        x3 = r["xt"][:, :, 0:n]
                        hm_ = r["hm"]
                        A_, B_ = cs(l, kind, 1, 0), cs(l, kind, 1, 1)
                        units = []

                        def u_load():
                            src = xview(srcX, t0, n) if kd == "x" else xview(srcC, 0, n)
                            P.dma("sp", x3, src, writes=["xt" + pz])
                            if kd == "x":
                                P.dma("sp", r["rc"][:, 0:n], ropeC[:, t0:t0 + n], writes=["rc" + pz])
                                P.dma("sp", r["rs"][:, 0:n], ropeS[:, t0:t0 + n], writes=["rs" + pz])
                        units.append(u_load)

                        def u_sq():
                            for kc in range(KC):
                                P.op("act", lambda e: e.activation(hm_[:, kc, 0:n], x3[:, kc, :], AF.Square),
                                     reads=["xt" + pz], writes=["hm" + pz])

                        def u_st():
                            for kc in range(KC):
                                P.op("pe", lambda e: e.matmul(r["R"][:, 0:n], ones_bf[:], hm_[:, kc, 0:n],
                                                              start=(kc == 0), stop=(kc == KC - 1)),
                                     reads=["hm" + pz, "ones"], writes=[r["Rk"]])

                        def u_rs():
                            P.op("act", lambda e: e.activation(r["rt"][:, 0:n], r["R"][:, 0:n], AF.Sqrt,
                                                               bias=eps_sb[:, 0:1], scale=1.0 / D),
                                 reads=["eps"], writes=["rt" + pz, r["Rk"]])
                            P.op("dve", lambda e: e.reciprocal(r["rstd"][:, 0:n], r["rt"][:, 0:n]),
                                 reads=["rt" + pz], writes=["rstd" + pz])
                        units += [u_sq, u_st, u_rs]
                        for g0 in (0, 4):
                            def u_ap(g0=g0):
                                for kc in range(g0, g0 + 4):
                                    tb = r["nt"][kc % 2]
                                    tk = f"nt{pz}{kc % 2}"
                                    P.op("dve", lambda e: e.scalar_tensor_tensor(tb[:, 0:n], x3[:, kc, :], A_[:, kc:kc + 1],
                                                                                 r["rstd"][:, 0:n], ALU.mult, ALU.mult),
                                         reads=["xt" + pz, "rstd" + pz, "cs"], writes=[tk])
                                    P.op("act", lambda e: e.activation(hm_[:, kc, 0:n], tb[:, 0:n], AF.Identity,
                                                                       bias=B_[:, kc:kc + 1], scale=1.0),
                                         reads=[tk, "cs"], writes=["hm" + pz])
                            units.append(u_ap)
                        for c2 in range(4):
                            def u_ka(c2=c2):
                                for kc in range(KC):
                                    P.op("pe", lambda e: e.matmul(r["K"][:, 0:n], wkv[:, kc, c2 * 128:(c2 + 1) * 128], hm_[:, kc, 0:n],
                                                                  start=(kc == 0), stop=(kc == KC - 1)),
                                         reads=["wkva", "wkvb", "hm" + pz], writes=[r["Kk"]])
                            if kd == "x":
                                def u_kb(c2=c2):
                                    P.op("act", lambda e: e.activation(r["qb"][:, 0:n], r["K"][:, 0:n], AF.Identity),
                                         reads=[], writes=["qb" + pz, r["Kk"]])
                                    P.op("dve", lambda e: e.tensor_tensor(r["t1"][:, 0:n], r["K"][:, 0:n], r["rc"][:, 0:n], ALU.mult),
                                         reads=["rc" + pz], writes=["t1" + pz, r["Kk"]])

                                def u_kc(c2=c2):
                                    P.op("pe", lambda e: e.matmul(r["R"][:, 0:n], pm_bf, r["qb"][:, 0:n], start=True, stop=True),
                                         reads=["qb" + pz, "pmi"], writes=[r["Rk"]])

                                def u_kd(c2=c2):
                                    P.op("dve", lambda e: e.tensor_tensor(r["t2"][:, 0:n], r["R"][:, 0:n], r["rs"][:, 0:n], ALU.mult),
                                         reads=["rs" + pz], writes=["t2" + pz, r["Rk"]])
                                    for j in range(2):
                                        P.op("pool", lambda e: e.tensor_tensor(KTa[0:64, 2 * c2 + j, t0:t0 + n],
                                                                               r["t1"][64 * j:64 * j + 64, 0:n],
                                                                               r["t2"][64 * j:64 * j + 64, 0:n], ALU.add),
                                             reads=["t1" + pz, "t2" + pz], writes=[f"KT{c2}"])
                                units += [u_ka, u_kb, u_kc, u_kd]
                            else:
                                def u_kb(c2=c2):
                                    P.op("act", lambda e: e.activation(KCa[0:64, 2 * c2, 0:n], r["K"][0:64, 0:n], AF.Identity),
                                         reads=[], writes=["KCa", r["Kk"]])
                                    P.op("dve", lambda e: e.tensor_copy(KCa[0:64, 2 * c2 + 1, 0:n], r["K"][64:128, 0:n]),
                                         reads=[], writes=["KCa", r["Kk"]])
                                units += [u_ka, u_kb]
                        for bi in range(n // 128):
                            def u_va(bi=bi):
                                for kc in range(KC):
                                    P.op("pe", lambda e: e.matmul(r["V"][:], hm_[:, kc, bi * 128:(bi + 1) * 128], wkv[:, kc, 512:1024],
                                                                  start=(kc == 0), stop=(kc == KC - 1)),
                                         reads=["wkva", "wkvb", "hm" + pz], writes=[r["Vk"]])

                            def u_vb(bi=bi):
                                v3 = r["V"][:].rearrange("p (h d) -> p h d", d=64)
                                if kd == "x":
                                    gb = t0 // 128 + bi
                                    P.op("act", lambda e: e.activation(V_sb[:, gb, :, 0:64], v3, AF.Identity),
                                         reads=[], writes=["V", r["Vk"]])
                                else:
                                    P.op("act", lambda e: e.activation(VC_sb[:, bi, :, 0:64], v3, AF.Identity),
                                         reads=[], writes=["VC", r["Vk"]])
                            units += [u_va, u_vb]
                        return units

                    chains = [[], []]
                    if l + 1 < DEPTH:
                        wmx = [SB(p1, f"wmx{i}", [128, KC, 256], BF16) for i in range(2)]
                        mod_sbx = SB(p1, "mod_sbx", [128, 2 * 72], F32)
                        tmp8x = SB(p1, "tmp8x", [128, 8], F32)
                        chains.append(mod_units(l + 1, wmx, ["wmx0", "wmx1"], Bk[6], "B6", mod_sbx, tmp8x, 256))
                    per_tile = [m1_units(ti, R_[ti % 2]) for ti in range(len(kvtiles))]
                    for ti in range(len(kvtiles)):
                        ul = per_tile[ti]
                        if ti + 2 < len(kvtiles):
                            nxt = per_tile[ti + 2].pop(0)
                            ul.insert(len(ul) - 8, nxt)
                        chains[ti % 2] += ul
                    k_ = 0
                    while any(k_ < len(c_) for c_ in chains):
                        for c_ in chains:
                            if k_ < len(c_):
                                c_[k_]()
                        k_ += 1
                    P.new_epoch()
                m1s.close()

                with contextlib.ExitStack() as p2:
                    W2 = 256
                    wqu = SB(p2, "wqu", [128, KC, 1024], BF16)
                    wo = SB(p2, "wo", [128, KC, 1024], BF16)
                    xt2 = [SB(p2, f"xq{i}", [128, KC, W2], F32) for i in range(2)]
                    hm2 = SB(p2, "hm2", [128, KC, W2], BF16)
                    QTa = [SB(p2, f"QTa{i}", [80, NH, W2], BF16) for i in range(2)]
                    mixT = [SB(p2, f"mixT{i}", [128, KC, W2], BF16) for i in range(3)]
                    y_sb = SB(p2, "y_sb", [128, KC, W2], F32)
                    pA = SB(p2, "pA", [128, 384], F32)
                    pB = SB(p2, "pB", [128, 384], F32)
                    pD = SB(p2, "pD", [128, 256], F32)
                    dT = SB(p2, "dT", [128, 256], BF16)
                    ic_x = SB(p2, "ic_x", [128, 4 * 64], F32)
                    ic_c = SB(p2, "ic_c", [128, 4 * 256], F32)
                    Pb = [SB(p2, f"Pb{i}", [128, 1024], BF16) for i in range(3)]
                    mixA = SB(p2, "mixA", [128, 512], BF16)
                    rec = SB(p2, "rec", [128, 8], F32)
                    rc2 = SB(p2, "rc2", [128, W2], F32)
                    rs2 = SB(p2, "rs2", [128, W2], F32)
                    qb2 = SB(p2, "qb2", [128, W2], BF16)
                    t12 = SB(p2, "t12", [128, W2], F32)
                    t22 = SB(p2, "t22", [128, W2], F32)
                    bufs2 = {
                        "sq": [SB(p2, f"sq2{i}", [128, W2], BF16) for i in range(2)],
                        "nt": [SB(p2, f"nt2{i}", [128, W2], F32) for i in range(2)],
                        "rt": SB(p2, "rt2", [128, W2], F32),
                        "rstd": SB(p2, "rstd2", [128, W2], F32),
                        "stat_ps": Bk[7][:, 256:512],
                        "stat_key": "B7",
                    }
                    Oa, Ob = Bk[4], Bk[5]
                    trpA = Bk[4][:, 384:512].bitcast(BF16)
                    trpB = Bk[5][:, 384:512].bitcast(BF16)
                    pjA = Bk[6]
                    pjB = Bk[7][:, 0:256]
                    P.dma("pool", wqu[:, :, 0:512], win3[:, :, 0:512], writes=["wq"])
                    P.dma("pool", wqu[:, :, 512:1024], win3[:, :, 1536:2048], writes=["wu"])
                    wo3 = wout[l].rearrange("(kc p) n -> p kc n", p=128)
                    P.dma("pool", wo[:, 0:4, :], wo3[:, 0:4, :], writes=["woa"])
                    P.dma("pool", wo[:, 4:8, :], wo3[:, 4:8, :], writes=["wob"])
                    P.dma("sp", ic_x[:], icx, writes=["icx"])
                    P.dma("sp", ic_c[:], icc, writes=["icc"])

                    qtiles = [("x", (q_blk0 + 2 * i) * 128, 256) for i in range(q_nblk // 2)]
                    if dstC is not None:
                        qtiles.append(("c", 0, CTX))
                    NT = len(qtiles)
                    kb0 = kv_t0 // 128
                    kb1 = kb0 + 4 * kv_ntiles
                    scnt = [0]
                    sbs = {}

                    def rope2(n, dst_fn, dkeys):
                        P.op("act", lambda e: e.activation(qb2[:, 0:n], pjA[:, 0:n], AF.Identity), reads=[], writes=["qb2", "B6"])
                        P.op("pe", lambda e: e.matmul(pjB[:, 0:n], pm_bf, qb2[:, 0:n], start=True, stop=True),
                             reads=["qb2", "pmi"], writes=["B7"])
                        P.op("dve", lambda e: e.tensor_tensor(t12[:, 0:n], pjA[:, 0:n], rc2[:, 0:n], ALU.mult),
                             reads=["rc2"], writes=["t12", "B6"])
                        P.op("dve", lambda e: e.tensor_tensor(t22[:, 0:n], pjB[:, 0:n], rs2[:, 0:n], ALU.mult),
                             reads=["rs2"], writes=["t22", "B7"])
                        P.op("pool", lambda e: e.tensor_tensor(dst_fn(0), t12[0:64, 0:n], t22[0:64, 0:n], ALU.add),
                             reads=["t12", "t22"], writes=dkeys)
                        P.op("pool", lambda e: e.tensor_tensor(dst_fn(1), t12[64:128, 0:n], t22[64:128, 0:n], ALU.add),
                             reads=["t12", "t22"], writes=dkeys)

                    def plain2(n, dst_fn, dkeys):
                        P.op("act", lambda e: e.activation(dst_fn(0), pjA[0:64, 0:n], AF.Identity), reads=[], writes=list(dkeys) + ["B6"])
                        P.op("dve", lambda e: e.tensor_copy(dst_fn(1), pjA[64:128, 0:n]), reads=[], writes=list(dkeys) + ["B6"])

                    xP, xT = xt2[0], xt2[1]
                    bufsT = {"nt": [SB(p2, f"ntT{i}", [128, W2], F32) for i in range(2)],
                             "rt": SB(p2, "rtT", [128, W2], F32), "rstd": SB(p2, "rstdT", [128, W2], F32)}
                    pjP = Bk[7][:, 0:256]
                    pjQ = Bk[7][:, 256:512]
                    pjT = Bk[6][:, 0:256]
                    stT = Bk[6][:, 0:256]
                    pjU = Bk[6][:, 256:512]

                    def stats_units(src3, n, srckeys, B, sq_fn, nslots, sqkeys, stat_ap, stat_key):
                        units = []
                        for g0 in range(0, KC, nslots):
                            def ua(g0=g0):
                                for j in range(nslots):
                                    P.op("act", lambda e: e.activation(sq_fn(j), src3[:, g0 + j, :], AF.Square),
                                         reads=srckeys, writes=sqkeys)

                            def ub(g0=g0):
                                for j in range(nslots):
                                    kc = g0 + j
                                    P.op("pe", lambda e: e.matmul(stat_ap[:, 0:n], ones_bf[:], sq_fn(j),
                                                                  start=(kc == 0), stop=(kc == KC - 1)),
                                         reads=list(sqkeys) + ["ones"], writes=[stat_key])
                            units += [ua, ub]

                        def uc():
                            P.op("act", lambda e: e.activation(B["rt"][:, 0:n], stat_ap[:, 0:n], AF.Ln,
                                                               bias=eps_sb[:, 0:1], scale=1.0 / D),
                                 reads=["eps"], writes=[B["k"] + "rt", stat_key])
                            P.op("act", lambda e: e.activation(B["rstd"][:, 0:n], B["rt"][:, 0:n], AF.Exp, scale=-0.5),
                                 reads=[B["k"] + "rt"], writes=[B["k"] + "rstd"])
                        units.append(uc)
                        return units

                    bufs2["k"] = "P"
                    bufsT["k"] = "T"

                    def prep_units(i):
                        kd, t0, n = qtiles[i]
                        kind = 0 if kd == "x" else 1
                        x3 = xP[:, :, 0:n]
                        Q, qk_, qmk = QTa[i % 2], f"QT{i % 2}", f"QTm{i % 2}"
                        MT, mkp = mixT[i % 3], f"mixT{i % 3}p"
                        A_, B_ = cs(l, kind, 1, 0), cs(l, kind, 1, 1)
                        units = []

                        def u_load():
                            src = xview(srcX, t0, n) if kd == "x" else xview(srcC, 0, n)
                            P.dma("sp", x3, src, writes=["xP"])
                            if kd == "x":
                                P.dma("sp", rc2[:, 0:n], ropeC[:, t0:t0 + n], writes=["rc2"])
                                P.dma("sp", rs2[:, 0:n], ropeS[:, t0:t0 + n], writes=["rs2"])
                                P.dma("sp", Q[64:80, :, 0:n], qm_bf[:, t0:t0 + n].unsqueeze(1).to_broadcast([16, NH, n]), writes=[qmk])
                            else:
                                P.op("pool", lambda e: e.memset(Q[64:80, :, :], 0.0), writes=[qmk])
                        units.append(u_load)
                        units += stats_units(x3, n, ["xP"], bufs2, lambda j: hm2[:, j, 0:n], 8, ["hm2"], pjQ, "B7")
                        for g0 in (0, 4):
                            def u_ap(g0=g0):
                                for kc in range(g0, g0 + 4):
                                    tb = bufs2["nt"][kc % 2]
                                    tk = f"ntP{kc % 2}"
                                    P.op("dve", lambda e: e.scalar_tensor_tensor(tb[:, 0:n], x3[:, kc, :], A_[:, kc:kc + 1],
                                                                                 bufs2["rstd"][:, 0:n], ALU.mult, ALU.mult),
                                         reads=["xP", "Prstd", "cs"], writes=[tk])
                                    P.op("dve", lambda e: e.tensor_scalar(hm2[:, kc, 0:n], tb[:, 0:n], B_[:, kc:kc + 1], None, ALU.add),
                                         reads=[tk, "cs"], writes=["hm2"])
                            units.append(u_ap)
                        for c2 in range(4):
                            def u_qa(c2=c2):
                                for kc in range(KC):
                                    P.op("pe", lambda e: e.matmul(pjP[:, 0:n], wqu[:, kc, c2 * 128:(c2 + 1) * 128], hm2[:, kc, 0:n],
                                                                  start=(kc == 0), stop=(kc == KC - 1)),
                                         reads=["wq", "hm2"], writes=["B7"])
                            dst = lambda j, c2=c2: Q[0:64, 2 * c2 + j, 0:n]
                            if kd == "x":
                                def u_qb(c2=c2):
                                    P.op("dve", lambda e: e.tensor_copy(qb2[:, 0:n], pjP[:, 0:n]), reads=[], writes=["qb2", "B7"])
                                    P.op("dve", lambda e: e.tensor_tensor(t12[:, 0:n], pjP[:, 0:n], rc2[:, 0:n], ALU.mult),
                                         reads=["rc2"], writes=["t12", "B7"])

                                def u_qc(c2=c2):
                                    P.op("pe", lambda e: e.matmul(pjQ[:, 0:n], pm_bf, qb2[:, 0:n], start=True, stop=True),
                                         reads=["qb2", "pmi"], writes=["B7"])

                                def u_qd(c2=c2, dst=dst):
                                    P.op("dve", lambda e: e.tensor_tensor(t22[:, 0:n], pjQ[:, 0:n], rs2[:, 0:n], ALU.mult),
                                         reads=["rs2"], writes=["t22", "B7"])
                                    P.op("pool", lambda e: e.tensor_tensor(dst(0), t12[0:64, 0:n], t22[0:64, 0:n], ALU.add),
                                         reads=["t12", "t22"], writes=[qk_])
                                    P.op("pool", lambda e: e.tensor_tensor(dst(1), t12[64:128, 0:n], t22[64:128, 0:n], ALU.add),
                                         reads=["t12", "t22"], writes=[qk_])
                                units += [u_qa, u_qb, u_qc, u_qd]
                            else:
                                def u_qb(c2=c2, dst=dst):
                                    P.op("dve", lambda e: e.tensor_copy(dst(0), pjP[0:64, 0:n]), reads=[], writes=[qk_, "B7"])
                                    P.op("dve", lambda e: e.tensor_copy(dst(1), pjP[64:128, 0:n]), reads=[], writes=[qk_, "B7"])
                                units += [u_qa, u_qb]
                        Lr = 64 if kd == "x" else 256
                        R = n // Lr
                        Lp = Lr + 32
                        ict = ic_x if kd == "x" else ic_c
                        ick = "icx" if kd == "x" else "icc"
                        pA3 = pA[:, 0:R * Lp].rearrange("p (r t) -> p r t", t=Lp)
                        pB3 = pB[:, 0:R * Lp].rearrange("p (r t) -> p r t", t=Lp)
                        pD3 = pD[:, 0:n].rearrange("p (r t) -> p r t", t=Lr)
                        uunits = [(lambda: None) for _ in range(6)]
                        qunits = units
                        units = uunits
                        for g in range(4):
                            def u_pa(g=g):
                                if g == 0:
                                    P.op("dve", lambda e: e.memset(pA[:], 0.0), writes=["pA"])
                                for kc in range(KC):
                                    P.op("pe", lambda e: e.matmul(pjU[:, 0:n], wqu[:, kc, 512 + g * 128:512 + (g + 1) * 128], hm2[:, kc, 0:n],
                                                                  start=(kc == 0), stop=(kc == KC - 1)),
                                         reads=["wu", "hm2"], writes=["B6"])

                            def u_pb(g=g):
                                w_ = [2, 4, 8, 16][g]
                                u3 = pjU[:, 0:n].rearrange("p (r t) -> p r t", t=Lr)
                                P.op("dve", lambda e: e.tensor_copy(pA3[:, :, 16:16 + Lr], u3),
                                     reads=[], writes=["pA", "B6"])
                                cur, curk, oth, othk = pA3, "pA", pB3, "pB"
                                lo = 0
                                sft = 1
                                while sft < w_:
                                    lo += sft
                                    P.op("dve", lambda e: e.tensor_tensor(oth[:, :, lo:Lp], cur[:, :, lo:Lp], cur[:, :, lo - sft:Lp - sft], ALU.add),
                                         reads=[curk], writes=[othk])
                                    cur, curk, oth, othk = oth, othk, cur, curk
                                    sft *= 2
                                st0 = 16 + w_ // 2 - 1
                                icv = ict[:, g * Lr:(g + 1) * Lr].unsqueeze(1).to_broadcast([128, R, Lr])
                                P.op("dve", lambda e: e.tensor_tensor(pD3, cur[:, :, st0:st0 + Lr], icv, ALU.mult),
                                     reads=[curk, ick], writes=["pD"])
                                P.op("dve", lambda e: e.tensor_tensor(dT[:, 0:n], pD[:, 0:n], pjU[:, 0:n], ALU.subtract),
                                     reads=["pD"], writes=["dT", "B6"])
                                if w_ > 2:
                                    P.op("dve", lambda e: e.memset(pA[:], 0.0), reads=[], writes=["pA"])

                            def u_pc(g=g):
                                P.op("pe", lambda e: e.matmul(pjU[:, 0:n], wpl[:, g, :], dT[:, 0:n], start=True, stop=True),
                                     reads=["wpl", "dT"], writes=["B6"])

                            def u_pd(g=g):
                                P.op("dve", lambda e: e.tensor_scalar(MT[:, 4 + g, 0:n], pjU[:, 0:n], psc_sb[:, l * 4 + g:l * 4 + g + 1], None, ALU.mult),
                                     reads=["psc"], writes=[mkp, "B6"])
                            units += [u_pa, u_pb, u_pc, u_pd]
                        return qunits, uunits

                    def tail_units(i):
                        kd, t0, n = qtiles[i]
                        kind = 0 if kd == "x" else 1
                        x3 = xT[:, :, 0:n]
                        MT, mkp, mka = mixT[i % 3], f"mixT{i % 3}p", f"mixT{i % 3}a"
                        C_ = cs(l, kind, 1, 2)
                        units = []

                        def u_load():
                            src = xview(srcX, t0, n) if kd == "x" else xview(srcC, 0, n)
                            P.dma("sp", x3, src, writes=["xT"])
                        units.append(u_load)
                        for oc in range(8):
                            def u_oa(oc=oc):
                                for kc in range(KC):
                                    P.op("pe", lambda e: e.matmul(pjT[:, 0:n], wo[:, kc, oc * 128:(oc + 1) * 128], MT[:, kc, 0:n],
                                                                  start=(kc == 0), stop=(kc == KC - 1)),
                                         reads=["woa", "wob", mkp, mka], writes=["B6"])

                            def u_ob(oc=oc):
                                P.op("dve", lambda e: e.tensor_copy(y_sb[:, oc, 0:n], pjT[:, 0:n]), reads=[], writes=["y", "B6"])
                            units += [u_oa, u_ob]
                        nwo = len(units)
                        y3 = y_sb[:, :, 0:n]
                        units += stats_units(y3, n, ["y"], bufsT, lambda j: MT[:, j, 0:n], 8, [mka, mkp], stT, "B6")
                        for g0 in (0, 4):
                            def u_ap(g0=g0):
                                for kc in range(g0, g0 + 4):
                                    tb = bufsT["nt"][kc % 2]
                                    tk = f"ntT{kc % 2}"
                                    P.op("dve", lambda e: e.scalar_tensor_tensor(tb[:, 0:n], y3[:, kc, :], C_[:, kc:kc + 1],
                                                                                 bufsT["rstd"][:, 0:n], ALU.mult, ALU.mult),
                                         reads=["y", "Trstd", "cs"], writes=[tk])
                                    P.op("pool", lambda e: e.tensor_tensor(x3[:, kc, :], x3[:, kc, :], tb[:, 0:n], ALU.add),
                                         reads=[tk], writes=["xT"])
                            units.append(u_ap)
                        dst = xview(dstX, t0, n) if kd == "x" else xview(dstC, 0, n)
                        units.append(lambda: P.dma("sp", dst, x3, reads=["xT"]))
                        return units, nwo

                    def zip_units(tu, pu):
                        out = []
                        k = 0
                        while k < len(tu) or k < len(pu):
                            if k < len(tu):
                                out.append(tu[k])
                            if k < len(pu):
                                out.append(pu[k])
                            k += 1
                        return out

                    def attn_units(i):
                        kd, t0, n = qtiles[i]
                        Q, qk_, qmk = QTa[i % 2], f"QT{i % 2}", f"QTm{i % 2}"
                        MT, mka = mixT[i % 3], f"mixT{i % 3}a"
                        units = []
                        pending_fin_b = []
                        for bi in range(n // 128):
                            b_ = 0
                            if kd == "x":
                                b_ = t0 // 128 + bi
                                lo_ = 0 if b_ == 19 else 1
                                hi_ = 7 if b_ == 4 else 6
                                lo_c = max(lo_, kb0 + 3 - b_)
                                hi_c = min(hi_, kb1 + 3 - b_)
                            else:
                                lo_c, hi_c = 0, 0
                            nl = hi_c - lo_c
                            assert nl <= 6

                            def qk_exp(h, sb_, pb_, bi=bi, b_=b_, lo_c=lo_c, hi_c=hi_c, nl=nl):
                                q_ap = Q[:, h, bi * 128:(bi + 1) * 128]
                                bk = [Bk[2 * sb_], Bk[2 * sb_ + 1]]
                                kk = [f"B{2 * sb_}", f"B{2 * sb_ + 1}"]
                                if nl > 0:
                                    w0 = min(nl, 4) * 128
                                    P.op("pe", lambda e: e.matmul(bk[0][:, 0:w0], ident8_bf, TTx[:, h, lo_c * 128:lo_c * 128 + w0],
                                                                  start=True, stop=False),
                                         reads=["TTx", "TTxb", "pmi"], writes=[kk[0]])
                                    if nl > 4:
                                        w1 = (nl - 4) * 128
                                        P.op("pe", lambda e: e.matmul(bk[1][:, 0:w1], ident8_bf, TTx[:, h, (lo_c + 4) * 128:(lo_c + 4) * 128 + w1],
                                                                      start=True, stop=False),
                                             reads=["TTx", "TTxb", "pmi"], writes=[kk[1]])
                                for ci in range(lo_c, hi_c):
                                    m = b_ - 3 + ci
                                    s_ = ci - lo_c
                                    P.op("pe", lambda e: e.matmul(bk[s_ // 4][:, (s_ % 4) * 128:(s_ % 4 + 1) * 128],
                                                                  KTa[:, h, m * 128:(m + 1) * 128], q_ap, start=False, stop=True),
                                         reads=["KT0", "KT1", "KT2", "KT3", qk_, qmk], writes=[kk[s_ // 4]])
                                for cm in range(2):
                                    P.op("pe", lambda e: e.matmul(bk[1][:, 256 + cm * 128:256 + (cm + 1) * 128],
                                                                  KCa[:, h, cm * 128:(cm + 1) * 128], q_ap, start=True, stop=True),
                                         reads=["KCa", qk_, qmk], writes=[kk[1]])
                                if nl > 0:
                                    w0 = min(nl, 4) * 128
                                    P.op("act", lambda e: e.activation(Pb[pb_][:, 0:w0], bk[0][:, 0:w0], AF.Exp, scale=0.125),
                                         reads=[], writes=[f"Pb{pb_}a", kk[0]])
                                if nl > 4:
                                    P.op("act", lambda e: e.activation(Pb[pb_][:, 512:1024], bk[1][:, 0:512], AF.Exp, scale=0.125),
                                         reads=[], writes=[f"Pb{pb_}b", kk[1]])
                                else:
                                    P.op("act", lambda e: e.activation(Pb[pb_][:, 768:1024], bk[1][:, 256:512], AF.Exp, scale=0.125),
                                         reads=[], writes=[f"Pb{pb_}b", kk[1]])

                            def pv(h, pb_, b_=b_, lo_c=lo_c, hi_c=hi_c):
                                O_ = Oa if h < 4 else Ob
                                okey = "B4" if h < 4 else "B5"
                                ocol = (h % 4) * 65
                                items = [("l", ci) for ci in range(lo_c, hi_c)] + [("c", 0), ("c", 1)]
                                for ii, (tp, ci) in enumerate(items):
                                    if tp == "l":
                                        m = b_ - 3 + ci
                                        s_ = ci - lo_c
                                        lhs = Pb[pb_][:, s_ * 128:(s_ + 1) * 128]
                                        rhs = V_sb[:, m, h, :]
                                    else:
                                        lhs = Pb[pb_][:, 768 + ci * 128:768 + (ci + 1) * 128]
                                        rhs = VC_sb[:, ci, h, :]
                                    P.op("pe", lambda e: e.matmul(O_[:, ocol:ocol + 65], lhs, rhs, start=(ii == 0), stop=(ii == len(items) - 1)),
                                         reads=[f"Pb{pb_}a", f"Pb{pb_}b", "V", "VC"], writes=[okey])

                            def fin_a(bi=bi):
                                Oa3 = Oa[:, 0:260].rearrange("p (h d) -> p h d", d=65)
                                Ob3 = Ob[:, 0:260].rearrange("p (h d) -> p h d", d=65)
                                P.op("act", lambda e: e.activation(rec[:, 0:4], Oa3[:, :, 64], AF.Ln), reads=[], writes=["rec", "B4"])
                                P.op("act", lambda e: e.activation(rec[:, 4:8], Ob3[:, :, 64], AF.Ln), reads=[], writes=["rec", "B5"])
                                P.op("act", lambda e: e.activation(rec[:, 0:8], rec[:, 0:8], AF.Exp, scale=-1.0), reads=["rec"], writes=["rec"])
                                for h in range(NH):
                                    O3 = Oa3 if h < 4 else Ob3
                                    okey = "B4" if h < 4 else "B5"
                                    P.op("act", lambda e: e.activation(mixA[:, h * 64:(h + 1) * 64], O3[:, h % 4, 0:64], AF.Identity,
                                                                       scale=rec[:, h:h + 1]),
                                         reads=["rec"], writes=["mixA", okey])

                            def fin_b(bi=bi):
                                for c4 in range(4):
                                    trp, tkey = (trpA, "B4") if c4 < 2 else (trpB, "B5")
                                    P.op("pe", lambda e: e.transpose(trp[:, (c4 % 2) * 128:(c4 % 2 + 1) * 128], mixA[:, c4 * 128:(c4 + 1) * 128], ident_bf),
                                         reads=["mixA", "pmi"], writes=[tkey])
                                P.op("act", lambda e: e.activation(MT[:, 0:2, bi * 128:(bi + 1) * 128],
                                                                   trpA.rearrange("p (c q) -> p c q", q=128), AF.Identity),
                                     reads=[], writes=[mka, "B4"])
                                P.op("act", lambda e: e.activation(MT[:, 2:4, bi * 128:(bi + 1) * 128],
                                                                   trpB.rearrange("p (c q) -> p c q", q=128), AF.Identity),
                                     reads=[], writes=[mka, "B5"])

                            def step(k, qk_exp=qk_exp, pv=pv):
                                def f_():
                                    if k < NH:
                                        sb_ = scnt[0] % 2
                                        pb_ = scnt[0] % 3
                                        scnt[0] += 1
                                        sbs[k] = pb_
                                        qk_exp(k, sb_, pb_)
                                    if k >= 2:
                                        pv(k - 2, sbs[k - 2])
                                return f_
                            blk_units = [step(k) for k in range(NH + 2)]
                            if pending_fin_b:
                                blk_units[2:2] = [pending_fin_b.pop()]
                            units += blk_units
                            units.append(fin_a)
                            pending_fin_b.append(fin_b)
                        while pending_fin_b:
                            units.append(pending_fin_b.pop())
                        return units

                    def interleave(au, bu):
                        na, nb = len(au), len(bu)
                        ia = ib = 0
                        while ia < na or ib < nb:
                            if ib >= nb or (ia < na and ia * nb <= ib * na):
                                au[ia]()
                                ia += 1
                            else:
                                bu[ib]()
                                ib += 1

                    def zip3(chains):
                        out = []
                        k = 0
                        while any(k < len(c) for c in chains):
                            for c in chains:
                                if k < len(c):
                                    out.append(c[k])
                            k += 1
                        return out

                    q0, u0 = prep_units(0)
                    for u in zip3([q0, u0]):
                        u()
                    carry = []
                    for i in range(NT):
                        tu = tail_units(i - 1)[0] if i > 0 else []
                        qu, uu = prep_units(i + 1) if i + 1 < NT else ([], [])
                        ncar = 3 if (tu and i + 1 < NT) else 0
                        aux = qu[0:1] + carry + zip3([tu[:len(tu) - ncar], qu[1:], uu])
                        carry = tu[len(tu) - ncar:] if ncar else []
                        interleave(attn_units(i), aux)
                    for u in carry + tail_units(NT - 1)[0]:
                        u()
                    P.new_epoch()

        def xt_tiles(src, dst, t0, ntiles, dst_off=0):
            return [dict(src=xview(src, t0 + 512 * i, 512), dst=xview(dst, t0 + 512 * i - dst_off, 512), n=512, kind=0)
                    for i in range(ntiles)]

        def c_tile(src, dst):
            return dict(src=xview(src, 0, CTX), dst=xview(dst, 0, CTX), n=CTX, kind=1)

        def net():
            tl = xt_tiles(XS[0], XS[1], 0, 6)
            yield lambda: ffn_phase(0, 0, [tl[0:2], tl[2:4], tl[4:6] + [c_tile(CS[0], CS[1])]])
            yield lambda: mix_phase(0, XS[1], XS[2], 0, 6, 2, 20, CS[1], CS[2])
            tl2 = xt_tiles(XS[2], XS[3], 256, 5)
            yield lambda: ffn_phase(0, 1, [tl2[0:2], tl2[2:4], tl2[4:5] + [c_tile(CS[2], CS[3])]])
            tl3 = xt_tiles(XS[3], XS[4], 256, 5)
            yield lambda: ffn_phase(1, 0, [tl3[0:2], tl3[2:4], tl3[4:5] + [c_tile(CS[3], CS[4])]])
            yield lambda: mix_phase(1, XS[4], XS[5], 256, 5, 4, 16, CS[4], None)
            tl4 = xt_tiles(XS[5], outT, 512, 4, dst_off=512)
            yield lambda: ffn_phase(1, 1, [tl4[0:2], tl4[2:4]])
        for i, ph_ in enumerate(net()):
            if i >= nphase:
                break
            ph_()
        for e in P.ENG:
            P.drain(e)
        print("ops", P.nops)
    return nc


def _consts(core):
    s = core % 4
    R0 = 32 * s
    lrow = np.arange(WROWS)
    grow = R0 - 8 + lrow
    inv = (np.float32(10000.0) ** (-np.arange(16, dtype=np.float32) / np.float32(16))).astype(np.float32)
    tok_row = np.repeat(grow, GW).astype(np.float32)
    tok_col = np.tile(np.arange(GW), WROWS).astype(np.float32)
    ang_r = (tok_row[:, None] * inv).astype(np.float32)
    ang_c = (tok_col[:, None] * inv).astype(np.float32)
    cos64 = np.concatenate([np.cos(ang_r), np.cos(ang_r), np.cos(ang_c), np.cos(ang_c)], axis=1)
    sin64 = np.concatenate([-np.sin(ang_r), np.sin(ang_r), -np.sin(ang_c), np.sin(ang_c)], axis=1)
    ropeC = np.ascontiguousarray(np.concatenate([cos64, cos64], axis=1).T.astype(np.float32))
    ropeS = np.ascontiguousarray(np.concatenate([sin64, sin64], axis=1).T.astype(np.float32))
    kmask = np.zeros((16, WTOK), np.float32)
    for l_ in range(WROWS):
        kmask[l_ % 16, l_ * GW:(l_ + 1) * GW] = 1.0
    qmask = np.full((16, WTOK), NEG, np.float32)
    for lq in range(WROWS):
        gq = grow[lq]
        if gq < 0 or gq >= ROWS:
            continue
        b = lq // 2
        rs_ = min(max(gq - 4, 0), ROWS - 8)
        for lk in range(2 * b - 6, 2 * b + 8):
            if lk < 0 or lk >= WROWS:
                continue
            gk = R0 - 8 + lk
            if 0 <= gk < ROWS and rs_ <= gk < rs_ + 8:
                qmask[lk % 16, lq * GW:(lq + 1) * GW] = 0.0
    return ropeC, ropeS, kmask, qmask


def _pool_icnt(length):
    t = np.arange(length)
    out = []
    for w in (2, 4, 8, 16):
        lo = np.clip(t - w // 2, 0, length)
        hi = np.clip(t - w // 2 + w, 0, length)
        out.append((np.float32(1.0) / (hi - lo).astype(np.float32)).astype(np.float32))
    return np.tile(np.concatenate(out)[None, :], (128, 1)).astype(np.float32)


def _bias_tables(na_rpb):
    j = np.arange(GW)
    col_start = np.clip(j - 8, 0, GW - 16)
    c = np.arange(GW)
    colvalid = (c[:, None] >= col_start[None, :]) & (c[:, None] < col_start[None, :] + 16)
    dc = np.clip(c[:, None] - j[None, :] + 15, 0, 30)
    bt = np.full((DEPTH, NH, 128, 896), NEG, np.float32)
    for mi in range(7):
        d0 = 2 * (mi - 3)
        for pk in range(2):
            for pq in range(2):
                dr = d0 + pk - pq
                if abs(dr) > 7:
                    continue
                vals = na_rpb[:, :, dr + 7, :][:, :, dc]
                vals = np.where(colvalid[None, None], vals, np.float32(NEG))
                bt[:, :, pk * 64:(pk + 1) * 64, mi * 128 + pq * 64: mi * 128 + (pq + 1) * 64] = vals
    return bt


_NC_CACHE = {}


def prepare_inputs(x, c, ctx, c_ctx, w_mod, b_mod, norm_g, w_ffn_gate_up, w_ffn_down, w_in, w_out, na_rpb,
                   w_pool, pool_scale):
    f = np.float32
    x = np.asarray(x, f); c = np.asarray(c, f); ctx = np.asarray(ctx, f); c_ctx = np.asarray(c_ctx, f)
    w_mod = np.ascontiguousarray(np.asarray(w_mod, f)); b_mod = np.asarray(b_mod, f); norm_g = np.asarray(norm_g, f)
    wgu = np.ascontiguousarray(np.asarray(w_ffn_gate_up, f)); wd = np.ascontiguousarray(np.asarray(w_ffn_down, f))
    w_in = np.ascontiguousarray(np.asarray(w_in, f)); w_out = np.ascontiguousarray(np.asarray(w_out, f))
    na_rpb = np.asarray(na_rpb, f); w_pool = np.ascontiguousarray(np.asarray(w_pool, f)); pool_scale = np.asarray(pool_scale, f)
    bm = np.ascontiguousarray(b_mod.reshape(DEPTH, 72, 128).transpose(2, 0, 1).reshape(128, DEPTH * 72))
    ng = np.ascontiguousarray(norm_g.reshape(DEPTH, 6 * 8, 128).transpose(2, 0, 1).reshape(128, DEPTH * 48))
    psc = np.ascontiguousarray(pool_scale.reshape(DEPTH, 4, 128).transpose(2, 0, 1).reshape(128, DEPTH * 4))
    bt = _bias_tables(na_rpb)
    icx = _pool_icnt(64)
    icc = _pool_icnt(256)
    pm = np.zeros((128, 128), f)
    for m in range(128):
        k = m + 16 if (m % 32) < 16 else m - 16
        pm[k, m] = 1.0
    pmi = np.ascontiguousarray(np.concatenate([pm, np.eye(128, dtype=f), 8.0 * np.eye(128, dtype=f)], axis=1))
    in_maps = []
    for core in range(8):
        b = core // 4
        R0 = 32 * (core % 4)
        xw = np.zeros((WROWS, GW, D), f)
        g0, g1 = R0 - 8, R0 - 8 + WROWS
        a0, a1 = max(g0, 0), min(g1, ROWS)
        xw[a0 - g0:a1 - g0] = x[b].reshape(ROWS, GW, D)[a0:a1]
        xT = np.ascontiguousarray(xw.reshape(WTOK, D).T)
        cT = np.ascontiguousarray(ctx[b].T)
        sT = np.ascontiguousarray(np.stack([c[b].reshape(8, 128).T, c_ctx.reshape(8, 128).T], axis=2).reshape(128, 16))
        ropeC, ropeS, kmask, qmask = _consts(core)
        in_maps.append(dict(xT=xT, cT=cT, sT=sT, wmod=w_mod, bm=bm, ng=ng, wgu=wgu, wd=wd, win=w_in, wout=w_out,
                            wpool=w_pool, psc=psc, bt=bt, ropeC=ropeC, ropeS=ropeS, kmask=kmask, qmask=qmask,
                            icx=icx, icc=icc, pmi=pmi))
    return in_maps


def kernel(**inputs):
    in_maps = prepare_inputs(**inputs)
    if "nc" not in _NC_CACHE:
        _NC_CACHE["nc"] = build()
    nc = _NC_CACHE["nc"]
    res = run_bass_kernel_spmd(nc, in_maps, core_ids=list(range(8)))
    out = np.zeros((2, 8192, D), np.float32)
    for core in range(8):
        b = core // 4
        R0 = 32 * (core % 4)
        o = np.asarray(res.results[core]["outT"], np.float32)
        out[b, R0 * GW:(R0 + 32) * GW, :] = o.T
    return out
```
